# Optimizing a Trainium2 kernel written in Bass

```python
import math
import jax, jax.numpy as jnp
from jax import lax
import numpy as np

D_MODEL = 1024
BATCH = 8
SEQ = 4096
DEPTH = 1

MEM_LEN = 256
GLA_HEADS = 4
GLA_DK = D_MODEL // 8
GLA_DV = D_MODEL // 4
GLA_QK = GLA_HEADS * GLA_DK
GLA_VW = GLA_HEADS * GLA_DV
GLA_RANK = 16
GLA_GATE_TEMP = 16.0
CHUNK = 64
POOL_GROUPS = 4
POOL_WIDTH = D_MODEL // 2
POOL_GC = POOL_WIDTH // POOL_GROUPS
POOL_WINDOWS = (2, 4, 8, 16)
XA_HEADS = 4
XA_HD = D_MODEL // XA_HEADS
D_FF = 2816
EPS = 1e-6
IN_SIZES = (GLA_QK, GLA_QK, GLA_VW, GLA_VW, GLA_RANK, POOL_WIDTH, D_MODEL, D_MODEL)
IN_TOTAL = sum(IN_SIZES)

kernel_name = "hybrid_gla_pool_gated_macaron"


def rmsnorm(x, g):
    x32 = x.astype(jnp.float32)
    r = lax.rsqrt(jnp.mean(x32 * x32, axis=-1, keepdims=True) + EPS)
    return (x32 * r * g.astype(jnp.float32)).astype(x.dtype)


def swiglu(h, w1, w3, w2):
    return (jax.nn.silu(h @ w1) * (h @ w3)) @ w2


def split_cols(t, sizes):
    out, o = [], 0
    for s in sizes:
        out.append(t[..., o:o + s])
        o += s
    return out


def gla_chunked(q, k, v, log_a):
    B, H, S, dk = q.shape
    dv = v.shape[-1]
    n_chunks = S // CHUNK

    def to_chunks(t):
        return jnp.moveaxis(t.astype(jnp.float32).reshape(B, H, n_chunks, CHUNK, t.shape[-1]), 2, 0)

    qc, kc, vc, ac = to_chunks(q), to_chunks(k), to_chunks(v), to_chunks(log_a)
    causal = jnp.tril(jnp.ones((CHUNK, CHUNK), dtype=bool))[None, None, :, :, None]

    def step(state, inp):
        qi, ki, vi, ai = inp
        b = jnp.cumsum(ai, axis=2)
        diff = b[:, :, :, None, :] - b[:, :, None, :, :]
        decay = jnp.exp(jnp.where(causal, diff, -jnp.inf))
        scores = jnp.einsum('bhijk,bhjk->bhij', qi[:, :, :, None, :] * decay, ki)
        o_intra = jnp.einsum('bhij,bhjv->bhiv', scores, vi)
        o_inter = jnp.einsum('bhik,bhkv->bhiv', qi * jnp.exp(b), state)
        b_last = b[:, :, -1:, :]
        k_dec = ki * jnp.exp(b_last - b)
        new_state = jnp.exp(b_last[:, :, 0, :])[..., None] * state + jnp.einsum('bhjk,bhjv->bhkv', k_dec, vi)
        return new_state, o_intra + o_inter

    state0 = jnp.zeros((B, H, dk, dv), jnp.float32)
    _, ys = lax.scan(step, state0, (qc, kc, vc, ac))
    return jnp.moveaxis(ys, 0, 2).reshape(B, H, S, dv).astype(v.dtype)


def trailing_mean(u, w):
    S = u.shape[1]
    cs = jnp.cumsum(u.astype(jnp.float32), axis=1)
    shifted = jnp.pad(cs, ((0, 0), (w, 0), (0, 0)))[:, :S]
    count = jnp.minimum(jnp.arange(1, S + 1), w).astype(jnp.float32)
    return ((cs - shifted) / count[None, :, None]).astype(u.dtype)


def hybrid_mixer(h, w_in, w_alpha, b_alpha, gla_head_norm, w_up_a, pool_mix, pool_scale, w_up_b, w_mix_out):
    B, S, _ = h.shape
    proj = h @ w_in
    q, k, v, r, a_code, u, ga, gb = split_cols(proj, IN_SIZES)

    log_a = jax.nn.log_sigmoid((a_code @ w_alpha + b_alpha).astype(jnp.float32)) / GLA_GATE_TEMP

    def heads(t, d):
        return t.reshape(B, S, GLA_HEADS, d).transpose(0, 2, 1, 3)

    o = gla_chunked(heads(q * (GLA_DK ** -0.5), GLA_DK), heads(k, GLA_DK), heads(v, GLA_DV), heads(log_a, GLA_DK))
    o = rmsnorm(o.transpose(0, 2, 1, 3), gla_head_norm)
    o = o.reshape(B, S, GLA_VW) * jax.nn.silu(r)
    y_a = o @ w_up_a

    ug = u.reshape(B, S, POOL_GROUPS, POOL_GC)
    pooled = jnp.stack([trailing_mean(ug[:, :, g, :], POOL_WINDOWS[g]) for g in range(POOL_GROUPS)], axis=2)
    z = jnp.einsum('bsgc,gcd->bsgd', pooled - ug, pool_mix)
    z = z * pool_scale.reshape(POOL_GROUPS, POOL_GC)
    y_b = z.reshape(B, S, POOL_WIDTH) @ w_up_b

    merged = jax.nn.sigmoid(ga) * y_a + jax.nn.sigmoid(gb) * y_b
    return merged @ w_mix_out


def cross_attention(h, m, wq, wk, wv, wo):
    B, S, _ = h.shape
    M = m.shape[1]
    q = (h @ wq).reshape(B, S, XA_HEADS, XA_HD)
    k = (m @ wk).reshape(B, M, XA_HEADS, XA_HD)
    v = (m @ wv).reshape(B, M, XA_HEADS, XA_HD)
    s = jnp.einsum('bshd,bmhd->bhsm', q, k).astype(jnp.float32) * (XA_HD ** -0.5)
    p = jax.nn.softmax(s, axis=-1).astype(v.dtype)
    o = jnp.einsum('bhsm,bmhd->bshd', p, v).reshape(B, S, D_MODEL)
    return o @ wo


def setup_inputs(seed: int = 0) -> dict:
    key = jax.random.key(seed)
    ks = iter(jax.random.split(key, 40))

    def nrm(shape, fan_in):
        return jax.random.normal(next(ks), shape, jnp.float32) * (fan_in ** -0.5)

    def gain(shape):
        return 1.0 + 0.02 * jax.random.normal(next(ks), shape, jnp.float32)

    L = DEPTH
    return {
        "x": jax.random.normal(next(ks), (BATCH, SEQ, D_MODEL), jnp.float32),
        "mem": jax.random.normal(next(ks), (BATCH, MEM_LEN, D_MODEL), jnp.float32),
        "ffn1_norm": gain((L, D_MODEL)),
        "ffn1_w1": nrm((L, D_MODEL, D_FF), D_MODEL),
        "ffn1_w3": nrm((L, D_MODEL, D_FF), D_MODEL),
        "ffn1_w2": nrm((L, D_FF, D_MODEL), D_FF),
        "mix_norm": gain((L, D_MODEL)),
        "w_in": nrm((L, D_MODEL, IN_TOTAL), D_MODEL),
        "w_alpha": nrm((L, GLA_RANK, GLA_QK), GLA_RANK),
        "b_alpha": 2.0 + 0.5 * jax.random.normal(next(ks), (L, GLA_QK), jnp.float32),
        "gla_head_norm": gain((L, GLA_DV)),
        "w_up_a": nrm((L, GLA_VW, D_MODEL), GLA_VW),
        "pool_mix": nrm((L, POOL_GROUPS, POOL_GC, POOL_GC), POOL_GC),
        "pool_scale": 0.5 + 0.1 * jax.random.normal(next(ks), (L, POOL_WIDTH), jnp.float32),
        "w_up_b": nrm((L, POOL_WIDTH, D_MODEL), POOL_WIDTH),
        "w_mix_out": nrm((L, D_MODEL, D_MODEL), D_MODEL),
        "xa_norm": gain((L, D_MODEL)),
        "mem_norm": gain((L, D_MODEL)),
        "xa_wq": nrm((L, D_MODEL, D_MODEL), D_MODEL),
        "xa_wk": nrm((L, D_MODEL, D_MODEL), D_MODEL),
        "xa_wv": nrm((L, D_MODEL, D_MODEL), D_MODEL),
        "xa_wo": nrm((L, D_MODEL, D_MODEL), D_MODEL),
        "ffn2_norm": gain((L, D_MODEL)),
        "ffn2_w1": nrm((L, D_MODEL, D_FF), D_MODEL),
        "ffn2_w3": nrm((L, D_MODEL, D_FF), D_MODEL),
        "ffn2_w2": nrm((L, D_FF, D_MODEL), D_FF),
        "final_norm": gain((D_MODEL,)),
    }


def reference(x, mem, ffn1_norm, ffn1_w1, ffn1_w3, ffn1_w2, mix_norm, w_in, w_alpha, b_alpha,
              gla_head_norm, w_up_a, pool_mix, pool_scale, w_up_b, w_mix_out, xa_norm, mem_norm,
              xa_wq, xa_wk, xa_wv, xa_wo, ffn2_norm, ffn2_w1, ffn2_w3, ffn2_w2, final_norm):
    for l in range(DEPTH):
        x = x + 0.5 * swiglu(rmsnorm(x, ffn1_norm[l]), ffn1_w1[l], ffn1_w3[l], ffn1_w2[l])
        x = x + hybrid_mixer(rmsnorm(x, mix_norm[l]), w_in[l], w_alpha[l], b_alpha[l], gla_head_norm[l],
                             w_up_a[l], pool_mix[l], pool_scale[l], w_up_b[l], w_mix_out[l])
        x = x + cross_attention(rmsnorm(x, xa_norm[l]), rmsnorm(mem, mem_norm[l]),
                                xa_wq[l], xa_wk[l], xa_wv[l], xa_wo[l])
        x = x + 0.5 * swiglu(rmsnorm(x, ffn2_norm[l]), ffn2_w1[l], ffn2_w3[l], ffn2_w2[l])
    return rmsnorm(x, final_norm)
```

```python
import numpy as np
import concourse.bass as bass
import concourse.mybir as mybir
from concourse.bass_utils import run_bass_kernel_spmd

F32 = mybir.dt.float32
BF16 = mybir.dt.bfloat16
AF = mybir.ActivationFunctionType
ALU = mybir.AluOpType

D = 1024
SEQ = 4096
T = 1024
SUB = 512
NSUB = T // SUB
DFF = 2816
NJ = DFF // 128
MEM = 256
IN_TOTAL = 5648
EPS = 1e-6
NSLOT = 5
SLOT_ELEMS = 4096

O_G_FFN1, O_G_MIX, O_G_XA, O_G_MEM, O_G_FFN2, O_G_FIN = 0, 8, 16, 24, 32, 40
O_GHEAD, O_PSCALE, O_CINV, O_BALPHA, O_IDENT, O_TRI = 48, 50, 54, 118, 630, 758
NS = 886


class Buf:
    __slots__ = ("name", "w", "r", "alias")

    def __init__(self, name):
        self.name = name
        self.w = None
        self.r = {}
        self.alias = []


class Sched:
    def __init__(self, nc):
        self.nc = nc
        self.names = ["pe", "act", "dve", "pool", "sp"]
        self.semh = {k: nc.alloc_semaphore(name="sem_" + k) for k in self.names}
        self.cnt = {k: 0 for k in self.names}
        self.known = {k: {} for k in self.names}
        self.prog = {k: [] for k in self.names}
        self.dmacnt = {}
        self.label = ""
        self.pe_labels = []

    def dma_sem(self, key):
        if key not in self.semh:
            self.semh[key] = self.nc.alloc_semaphore(name="sem_" + key.replace(":", "_"))
            self.dmacnt[key] = 0
        return key

    def _deps(self, eng, reads, writes):
        deps = {}

        def add(kv):
            if kv is None:
                return
            k, v = kv
            if deps.get(k, 0) < v:
                deps[k] = v

        for b in reads:
            add(b.w)
        for b in writes:
            add(b.w)
            for k, v in b.r.items():
                add((k, v))
            for a in b.alias:
                add(a.w)
                for k, v in a.r.items():
                    add((k, v))
        for k, v in deps.items():
            if k == "pe" and eng == "pe":
                continue
            if self.known[eng].get(k, 0) < v:
                self.known[eng][k] = v
                self.prog[eng].append(("wait", k, v))

    def op(self, eng, fn, reads=(), writes=()):
        self._deps(eng, reads, writes)
        self.cnt[eng] += 1
        val = self.cnt[eng]
        self.prog[eng].append(("ins", fn, True))
        for b in reads:
            b.r[eng] = val
        for b in writes:
            b.w = (eng, val)
            b.r = {}
        return val

    def mm(self, items, reads, writes):
        eng = "pe"
        self._deps(eng, reads, writes)
        self.cnt[eng] += 1
        val = self.cnt[eng]
        n = len(items)
        self.pe_labels.extend([self.label] * n)
        for i, (o, l, r, st, sp) in enumerate(items):
            self.prog[eng].append(
                ("ins", (lambda e, o=o, l=l, r=r, st=st, sp=sp: e.matmul(o, l, r, start=st, stop=sp)), i == n - 1)
            )
        for b in reads:
            b.r[eng] = val
        for b in writes:
            b.w = (eng, val)
            b.r = {}

    def tr(self, items, reads, writes):
        eng = "pe"
        self._deps(eng, reads, writes)
        self.cnt[eng] += 1
        val = self.cnt[eng]
        n = len(items)
        self.pe_labels.extend([self.label] * n)
        for i, (o, a, idn) in enumerate(items):
            self.prog[eng].append(("ins", (lambda e, o=o, a=a, idn=idn: e.transpose(o, a, idn)), i == n - 1))
        for b in reads:
            b.r[eng] = val
        for b in writes:
            b.w = (eng, val)
            b.r = {}

    def dma(self, queue, pairs, reads, writes, key):
        self.dma_sem(key)
        self._deps(queue, reads, writes)
        for (o, i) in pairs:
            self.dmacnt[key] += 16
            self.prog[queue].append(("dma", (lambda e, o=o, i=i: e.dma_start(out=o, in_=i)), key))
        val = self.dmacnt[key]
        for b in reads:
            b.r[key] = val
        for b in writes:
            b.w = (key, val)
            b.r = {}

    def wait_all(self, eng, bufs):
        self._deps(eng, bufs, bufs)

    def emit(self):
        nc = self.nc
        engs = {"pe": "tensor", "act": "scalar", "dve": "vector", "pool": "gpsimd", "sp": "sync"}
        with nc.Block() as block:
            for name in self.names:
                prog = self.prog[name]
                own = self.semh[name]

                def body(e, prog=prog, own=own):
                    for item in prog:
                        if item[0] == "wait":
                            e.wait_ge(self.semh[item[1]], item[2])
                        elif item[0] == "ins":
                            ins = item[1](e)
                            if item[2]:
                                ins.then_inc(own, 1)
                        else:
                            ins = item[1](e)
                            ins.then_inc(self.semh[item[2]], 16)

                getattr(block, engs[name])(body)


def build_program(nt=SEQ // T, stage=99):
    nc = bass.Bass("TRN2", target_bir_lowering=False)
    S = Sched(nc)

    def din(name, shape):
        return nc.dram_tensor(name, list(shape), F32, kind="ExternalInput").ap()

    x_d = din("x", [SEQ, D])
    mem_d = din("mem", [MEM, D])
    small_d = din("small", [128, NS])
    w1_d = [din("ffn1_w1", [D, DFF]), din("ffn2_w1", [D, DFF])]
    w3_d = [din("ffn1_w3", [D, DFF]), din("ffn2_w3", [D, DFF])]
    w2_d = [din("ffn1_w2", [DFF, D]), din("ffn2_w2", [DFF, D])]
    win_d = din("w_in", [D, IN_TOTAL])
    walpha_d = din("w_alpha", [16, 512])
    balpha_d = din("b_alpha", [1, 512])
    wupa_d = din("w_up_a", [D, D])
    pmix_d = din("pool_mix", [4, 128, 128])
    wupb_d = din("w_up_b", [512, D])
    wmix_d = din("w_mix_out", [D, D])
    wq_d = din("xa_wq", [D, D])
    wk_d = din("xa_wk", [D, D])
    wv_d = din("xa_wv", [D, D])
    wo_d = din("xa_wo", [D, D])
    out_d = nc.dram_tensor("out", [SEQ, D], F32, kind="ExternalOutput").ap()

    def kview(w):
        return w.rearrange("(kc p) n -> p kc n", p=128)

    def sb(name, shape, dt):
        return nc.alloc_sbuf_tensor("sb_" + name, list(shape), dt)

    xT = sb("xT", [128, 8, T], F32)
    hT = sb("hT", [128, 8, T], BF16)
    slots = [sb(f"slot{i}", [128, SLOT_ELEMS], BF16) for i in range(NSLOT)]
    small = sb("small", [128, NS], F32)
    KT = sb("KT", [128, 8, MEM], BF16)
    Vm = sb("Vm", [128, 2, D], BF16)
    S32 = sb("S32", [128, 4, 256], F32)
    Sbf2 = [sb(f"Sbf{i}", [128, 4, 256], BF16) for i in range(2)]
    identB = sb("identB", [128, 128], BF16)
    triB = sb("triB", [128, 128], BF16)
    onesD = sb("onesD", [128, 128], BF16)
    onesV = sb("onesV", [128, 128], BF16)
    ones1 = sb("ones1", [128, 128], BF16)
    walpha = sb("walpha", [17, 512], BF16)
    uhalo = sb("uhalo", [128, 4, 16], F32)
    uhalo_b = Buf("uhalo")
    ARENA = 97 * 1024
    arena = sb("arena", [128, ARENA // 4], F32)
    abase = nc.lookup_mloc(arena).addr

    overlay = []

    def ov(name, off_k, shape, dt, nb=1):
        esz = 4 if dt == F32 else 2
        size = int(np.prod(shape[1:])) * esz
        lo = int(off_k * 1024)
        hi = lo + size
        assert hi <= ARENA, (name, hi)
        t = nc.alloc_sbuf_tensor_at("ov_" + name, list(shape), dt, offset=abase + lo)
        bufs = [Buf(f"{name}{i}") for i in range(nb)]
        for (l2, h2, b2) in overlay:
            if l2 < hi and lo < h2:
                for b in bufs:
                    for o in b2:
                        b.alias.append(o)
                        o.alias.append(b)
        overlay.append((lo, hi, bufs))
        return t, bufs

    stg_o = [ov(f"stg{i}", 16 + 4 * i, [128, D], F32) for i in range(2)]
    stg = [x[0] for x in stg_o]
    NOST = 8
    ost_o = [ov(f"ost{i}", 2 * i, [128, 512], F32) for i in range(NOST)]
    gated, gated_b = ov("gated", 0, [128, NJ, T], BF16, nb=NJ * NSUB)
    sl_t = [ov(f"sl{i}", 44 + i, [128, SUB], BF16) for i in range(2)]
    sq, sq_b = ov("sq", 46, [128, 8, SUB], BF16)
    lnv, lnv_b = ov("lnv", 54, [128, SUB], F32)
    rstd, rstd_b = ov("rstd", 56, [128, SUB], F32)
    ptmp, ptmp_b = ov("ptmp", 36, [128, 4, SUB], F32)
    xpf = [ov(f"xpf{i}", 58 + 4 * i, [128, D], F32) for i in range(8)]
    qT, qT_b = ov("qT", 0, [128, 4, T], BF16, nb=8)
    kTt, kT_b = ov("kT", 8, [128, 4, T], BF16, nb=8)
    vT, v_b = ov("vtok", 16, [128, 8, D], BF16, nb=8)
    srT, sr_b = ov("srT", 32, [128, 8, T], BF16, nb=8)
    aT, aT_b = ov("aT", 48, [32, T], BF16, nb=1)
    zb2 = [ov(f"zb{i}", 50 + 2 * i, [128, 512], F32) for i in range(2)]
    lp2 = [ov(f"lp{i}", 54 + i, [128, 512], BF16) for i in range(2)]
    enb2 = [ov(f"enb{i}", 56 + 2 * i, [128, 4, 128], F32) for i in range(2)]
    eb2 = [ov(f"eb{i}", 60 + 2 * i, [128, 4, 128], F32) for i in range(2)]
    kgTa, kgTa_b = ov("kgTa", 64, [128, 8, 512], BF16, nb=8)
    smka, smka_b = ov("smka", 72, [128, 8, 512], BF16, nb=8)
    ebl, ebl_b = ov("ebl", 80, [128, 8, 4], F32, nb=8)
    osq, osq_b = ov("osq", 50, [128, 8, 128], BF16)
    rsn, rsn_b = ov("rsn", 52, [128, 4, 128], F32)
    onT, onT_b = ov("onT", 54, [128, 8, 128], BF16)
    ogT, og_b2 = ov("ogT", 81, [128, 8, T], BF16, nb=16)
    og_b = [[og_b2[2 * c], og_b2[2 * c + 1]] for c in range(8)]
    tga, tga_b = ov("tga", 0, [128, 8, T], BF16, nb=16)
    tgb, tgb_b = ov("tgb", 16, [128, 8, T], BF16, nb=16)
    uT, uT_b = ov("uT", 32, [128, 4, 16 + T], F32, nb=1)
    pwA, pwA_b = ov("pwA", 48.25, [128, 16 + SUB], F32)
    pwB, pwB_b = ov("pwB", 50.5, [128, 16 + SUB], F32)
    dfT, df_b = ov("dfT", 52.75, [128, 4, T], BF16, nb=8)
    zT, zT_b = ov("zT", 60.75, [128, 4, T], BF16, nb=8)
    t1, t1_b = ov("t1", 69, [128, SUB], F32)
    xq, xq_b = ov("xq", 0, [128, 8, T], BF16, nb=16)
    xo, xo_b = ov("xo", 16, [128, 8, T], BF16, nb=16)
    Pm = [ov(f"Pm{i}", 32 + i, [128, SUB], BF16) for i in range(4)]
    rz2 = [ov(f"rz{i}", 36 + 2 * i, [128, SUB], F32) for i in range(2)]
    memT, memT_b = ov("memT", 0, [128, 8, MEM], F32)
    mhT, mhT_b = ov("mhT", 8, [128, 8, MEM], BF16)

    psum_all = nc.alloc_psum_tensor("psum_all", [128, 8 * 512], F32)
    banks = [psum_all[:, i * 512:(i + 1) * 512] for i in range(8)]
    bank_b = [Buf(f"bank{i}") for i in range(8)]
    bank_rr = [0]

    psum_only7 = [False]

    def psum():
        if psum_only7[0]:
            return banks[7], bank_b[7]
        i = bank_rr[0] % 8
        bank_rr[0] += 1
        return banks[i], bank_b[i]

    def psum_at(i):
        return banks[i], bank_b[i]

    xT_b = [[Buf(f"xT{m}_{s}") for s in range(NSUB)] for m in range(8)]
    hT_b = [[Buf(f"hT{s}_{kc}") for kc in range(8)] for s in range(NSUB)]
    slot_b = [Buf(f"slot{i}") for i in range(NSLOT)]
    stg_b = [x[1][0] for x in stg_o]
    const_b = Buf("const")
    KT_b, Vm_b = Buf("KT"), Buf("Vm")
    S32_b = [Buf("S32a"), Buf("S32b")]
    Sbf_b = [[Buf(f"Sbf{p}{hh}") for hh in range(2)] for p in range(2)]
    slot_rr = [0]

    pre_issued = {}

    def wload(parts, key=None, slot=None):
        if key is not None and key in pre_issued:
            return pre_issued.pop(key)
        if slot is None:
            i = slot_rr[0] % NSLOT
            slot_rr[0] += 1
        else:
            i = slot
        st = slots[i]
        S.dma("pool", [(vf(st), src) for (vf, src) in parts], [], [slot_b[i]], f"dma:slot{i}")
        last_slot[0] = i
        return st, slot_b[i]

    last_slot = [0]

    def wpre(key, parts):
        pre_issued[key] = wload(parts)

    def v3(st, a, b):
        return st[:, 0:a * b].rearrange("p (a b) -> p a b", a=a)

    S.dma("sp", [(small[:], small_d)], [], [const_b], "dma:const")
    wa_b = Buf("walpha")
    S.dma("pool", [(walpha[0:16, :], walpha_d), (walpha[16:17, :], balpha_d)], [], [wa_b], "dma:const2")
    cb2 = Buf("const2")
    S.op("dve", lambda e: e.tensor_copy(out=identB[:], in_=small[:, O_IDENT:O_IDENT + 128]), [const_b], [cb2])
    S.op("dve", lambda e: e.tensor_copy(out=triB[:], in_=small[:, O_TRI:O_TRI + 128]), [const_b], [cb2])
    S.op("dve", lambda e: e.memset(onesD[:], 1.0 / 1024.0), [], [cb2])
    S.op("dve", lambda e: e.memset(onesV[:], 1.0 / 256.0), [], [cb2])
    S.op("dve", lambda e: e.memset(ones1[:], 1.0), [], [cb2])
    S.op("dve", lambda e: e.memset(S32[:], 0.0), [], S32_b)
    for p_ in range(2):
        S.op("dve", lambda e, p_=p_: e.memset(Sbf2[p_][:], 0.0), [], Sbf_b[p_])
    CB = [const_b, cb2, wa_b]
    identF = small[:, O_IDENT:O_IDENT + 128]

    def gvec(off, kc):
        return small[:, off + kc:off + kc + 1]

    sq_b2 = [Buf("sqA"), Buf("sqB")]
    for b_ in sq_b2:
        b_.alias = sq_b[0].alias

    def norm_pieces(src3, src_bufs_fn, ntok, goff, dst3, dst_bufs_fn, c):
        lo, hi = c * SUB, min(ntok, (c + 1) * SUB)
        w = hi - lo
        sb_ = src_bufs_fn(c)
        st = {}

        def p_sq():
            for hf in range(2):
                S.op("act", lambda e, hf=hf: e.activation(
                    out=sq[:, hf * 4:hf * 4 + 4, 0:w], in_=src3(lo, hi)[:, hf * 4:hf * 4 + 4, :], func=AF.Square),
                    sb_, [sq_b2[hf]])

        def p_ones():
            bk, bb = psum()
            st["bk"], st["bb"] = bk, bb
            S.mm([(bk[:, 0:w], onesD[:], sq[:, kc, 0:w], kc == 0, kc == 7) for kc in range(8)],
                 sq_b2 + CB, [bb])

        def p_rest():
            bk, bb = st["bk"], st["bb"]
            S.op("act", lambda e: e.activation(out=lnv[:, 0:w], in_=bk[:, 0:w], func=AF.Ln, bias=EPS),
                 [bb], [lnv_b[0]])
            S.op("act", lambda e: e.activation(out=rstd[:, 0:w], in_=lnv[:, 0:w], func=AF.Exp, scale=-0.5),
                 [lnv_b[0]], [rstd_b[0]])
            db_ = dst_bufs_fn(c)
            if len(db_) == 1:
                db_ = db_ * 8
            for kc in range(8):
                S.op("dve", lambda e, kc=kc: e.scalar_tensor_tensor(
                    out=dst3(lo, hi)[:, kc, :], in0=src3(lo, hi)[:, kc, :], scalar=gvec(goff, kc), in1=rstd[:, 0:w],
                    op0=ALU.mult, op1=ALU.mult), [rstd_b[0]] + CB + sb_, [db_[kc]])

        return p_sq, p_ones, p_rest

    deferred = []

    def run_deferred():
        while deferred:
            lab = S.label
            S.label = lab.split("/")[0] + "/dnorm"
            deferred.pop(0)()
            S.label = lab

    def rmsnorm_T(src3, src_bufs_fn, ntok, goff, dst3, dst_bufs_fn, chunks=None, defer_last=False):
        nch = (ntok + SUB - 1) // SUB
        cl = list(range(nch) if chunks is None else chunks)
        for c in cl:
            pcs = norm_pieces(src3, src_bufs_fn, ntok, goff, dst3, dst_bufs_fn, c)
            if defer_last and c == cl[-1]:
                pcs[0]()
                deferred.append(pcs[1])
                deferred.append(pcs[2])
            else:
                for p in pcs:
                    p()

    def xsrc(lo, hi):
        return xT[:, :, lo:hi]

    def xbufs(c):
        return [xT_b[m][c] for m in range(8)]

    hdst = lambda lo, hi: hT[:, :, lo:hi]
    hdb = lambda c: hT_b[c]

    def norm_x_to_h(goff, chunks=None):
        S.label = S.label.split("/")[0] + "/norm"
        rmsnorm_T(xsrc, xbufs, T, goff, hdst, hdb, chunks=chunks, defer_last=True)

    def tail_hook(goff, ngroups):
        pcs = norm_pieces(xsrc, xbufs, T, goff, hdst, hdb, 0)

        def hook(s, gi):
            lab = S.label
            S.label = lab.split("/")[0] + "/tailnorm"
            if s == 0 and gi == ngroups:
                pcs[0]()
            if s == 1 and gi == 3:
                pcs[1]()
                pcs[2]()
            S.label = lab
        return hook

    def proj_multi(specs, nchunk, kch, rhs_fn, rhs_bufs_fn, evac, hook=None):
        loaded = []
        for (wv_cols, key) in specs:
            st, sbuf = wload(proj_parts(wv_cols, nchunk, kch), key=key)
            loaded.append((v3(st, kch, nchunk * 128), sbuf))
        for s in range(NSUB):
            gi = 0
            for si, (sv, sbuf) in enumerate(loaded):
                for m in range(nchunk):
                    bk, bb = psum()
                    S.mm([(bk[:], sv[:, kc, m * 128:(m + 1) * 128], rhs_fn(kc, s), kc == 0, kc == kch - 1)
                          for kc in range(kch)], [sbuf] + rhs_bufs_fn(s), [bb])
                    evac(si, m, s, bk, bb)
                    gi += 1
                    if s == 0 and gi == 3:
                        run_deferred()
                    if hook is not None:
                        hook(s, gi)


    def ffn_parts(l, jp):
        w1v, w3v = kview(w1_d[l]), kview(w3_d[l])
        j0 = jp * 2
        return [
            (lambda st: v3(st, 8, 512)[:, :, 0:256], w1v[:, :, j0 * 128:(j0 + 2) * 128]),
            (lambda st: v3(st, 8, 512)[:, :, 256:512], w3v[:, :, j0 * 128:(j0 + 2) * 128]),
        ]

    def ffn_pre(l):
        for jp in range(2):
            wpre(("ffn", l, jp), ffn_parts(l, jp))

    def ffn(l, next_pcs=None):
        w1v, w3v, w2v = kview(w1_d[l]), kview(w3_d[l]), kview(w2_d[l])
        S.label = f"ffn{l}/p1"
        pairs = [(0, 1), (2, 3), (4, 5), (6, 7), (8, 9), (10,)]
        for pair in pairs:
            loaded = [wload(ffn_parts(l, jp), key=("ffn", l, jp)) for jp in pair]
            for s in range(NSUB):
                for jp, (st, sbuf) in zip(pair, loaded):
                    sv = v3(st, 8, 512)
                    for jj in range(2):
                        j = jp * 2 + jj
                        rhs = lambda kc, s=s: hT[:, kc, s * SUB:(s + 1) * SUB]
                        b1, bb1 = psum()
                        S.mm([(b1[:], sv[:, kc, jj * 128:(jj + 1) * 128], rhs(kc), kc == 0, kc == 7) for kc in range(8)],
                             [sbuf] + hT_b[s], [bb1])
                        b3, bb3 = psum()
                        S.mm([(b3[:], sv[:, kc, 256 + jj * 128:256 + (jj + 1) * 128], rhs(kc), kc == 0, kc == 7)
                              for kc in range(8)], [sbuf] + hT_b[s], [bb3])
                        slt, slb = sl_t[(j * NSUB + s) % 2]
                        S.op("act", lambda e, b1=b1, slt=slt: e.activation(out=slt[:], in_=b1[:], func=AF.Silu),
                             [bb1], [slb[0]])
                        gb = gated_b[j * NSUB + s]
                        S.op("dve", lambda e, b3=b3, slt=slt, j=j, s=s: e.tensor_tensor(
                            out=gated[:, j, s * SUB:(s + 1) * SUB], in0=b3[:], in1=slt[:], op=ALU.mult),
                            [bb3, slb[0]], [gb])
                        if s == 0 and j == 1:
                            run_deferred()
        S.label = f"ffn{l}/p2"

        def w2parts(m):
            return [(lambda st: v3(st, NJ, 128), w2v[:, :, m * 128:(m + 1) * 128])]

        def p2group(m, s, st, sbuf):
            sv = v3(st, NJ, 128)
            bk, bb = psum()
            S.mm([(bk[:], sv[:, j, :], gated[:, j, s * SUB:(s + 1) * SUB], j == 0, j == NJ - 1)
                  for j in range(NJ)], [sbuf] + [gated_b[j * NSUB + s] for j in range(NJ)], [bb])
            S.op("dve", lambda e, bk=bk, m=m, s=s: e.scalar_tensor_tensor(
                out=xT[:, m, s * SUB:(s + 1) * SUB], in0=bk[:], scalar=0.5, in1=xT[:, m, s * SUB:(s + 1) * SUB],
                op0=ALU.mult, op1=ALU.add), [bb], [xT_b[m][s]])

        held = {}
        for m in range(8):
            st, sbuf = wload(w2parts(m))
            held[m] = (st, sbuf, last_slot[0])
            p2group(m, 0, st, sbuf)
        if next_pcs is not None:
            lab = S.label
            S.label = lab.split("/")[0] + "/tailnorm"
            next_pcs[0]()
            S.label = lab
        resident = set(range(8 - NSLOT, 8))
        reload_q = [m for m in range(8 - NSLOT - 1, -1, -1)]
        for idx, m in enumerate(range(7, -1, -1)):
            st, sbuf, si = held[m]
            p2group(m, 1, st, sbuf)
            if m in resident and reload_q:
                m2 = reload_q.pop(0)
                st2, sbuf2 = wload(w2parts(m2), slot=si)
                held[m2] = (st2, sbuf2, si)
            if idx == 1 and next_pcs is not None:
                lab = S.label
                S.label = lab.split("/")[0] + "/tailnorm"
                next_pcs[1]()
                next_pcs[2]()
                S.label = lab

    def proj_parts(wv_cols, nchunk, kch):
        return [(lambda st: v3(st, kch, nchunk * 128), wv_cols)]

    def proj_T(wv_cols, nchunk, kch, rhs_fn, rhs_bufs_fn, evac, key=None):
        st, sbuf = wload(proj_parts(wv_cols, nchunk, kch), key=key)
        sv = v3(st, kch, nchunk * 128)
        for m in range(nchunk):
            for s in range(NSUB):
                bk, bb = psum()
                S.mm([(bk[:], sv[:, kc, m * 128:(m + 1) * 128], rhs_fn(kc, s), kc == 0, kc == kch - 1)
                      for kc in range(kch)], [sbuf] + rhs_bufs_fn(s), [bb])
                evac(m, s, bk, bb)

    hrhs = lambda kc, s: hT[:, kc, s * SUB:(s + 1) * SUB]
    hrb = lambda s: hT_b[s]

    def sl(s):
        return slice(s * SUB, (s + 1) * SUB)

    def mem_prologue():
        S.label = "mem"
        for blk in range(2):
            S.dma("sp", [(stg[blk][:], mem_d[blk * 128:(blk + 1) * 128, :])], [], [stg_b[blk]], f"dma:stg{blk}")
            for half in range(2):
                bk, bb = psum()
                S.tr([(bk[:, i * 128:(i + 1) * 128], stg[blk][:, (half * 4 + i) * 128:(half * 4 + i + 1) * 128], identF)
                      for i in range(4)], [stg_b[blk]] + CB, [bb])
                S.op("act", lambda e, bk=bk, half=half, blk=blk: e.activation(
                    out=memT[:, half * 4:half * 4 + 4, blk * 128:(blk + 1) * 128],
                    in_=bk[:].rearrange("p (a b) -> p a b", a=4), func=AF.Copy), [bb], [memT_b[0]])
        rmsnorm_T(lambda lo, hi: memT[:, :, lo:hi], lambda c: [memT_b[0]], MEM, O_G_MEM,
                  lambda lo, hi: mhT[:, :, lo:hi], lambda c: [mhT_b[0]])
        wkv = kview(wk_d)
        for half in range(2):
            st, sbuf = wload([(lambda st: v3(st, 8, 512), wkv[:, :, half * 512:(half + 1) * 512])])
            sv = v3(st, 8, 512)
            for c in range(4):
                bk, bb = psum()
                S.mm([(bk[:, 0:MEM], sv[:, kc, c * 128:(c + 1) * 128], mhT[:, kc, :], kc == 0, kc == 7)
                      for kc in range(8)], [sbuf, mhT_b[0]], [bb])
                S.op("act", lambda e, bk=bk, half=half, c=c: e.activation(
                    out=KT[:, half * 4 + c, :], in_=bk[:, 0:MEM], func=AF.Copy), [bb], [KT_b])
        wvv = kview(wv_d)
        for half in range(2):
            st, sbuf = wload([(lambda st: v3(st, 8, 512), wvv[:, :, half * 512:(half + 1) * 512])])
            sv = v3(st, 8, 512)
            for mb in range(2):
                bk, bb = psum()
                S.mm([(bk[:], mhT[:, kc, mb * 128:(mb + 1) * 128], sv[:, kc, :], kc == 0, kc == 7)
                      for kc in range(8)], [sbuf, mhT_b[0]], [bb])
                S.op("act", lambda e, bk=bk, half=half, mb=mb: e.activation(
                    out=Vm[:, mb, half * 512:(half + 1) * 512], in_=bk[:], func=AF.Copy), [bb], [Vm_b])

    def prefetch_tile(t):
        for blk in range(T // 128):
            r0 = t * T + blk * 128
            S.dma("sp", [(xpf[blk][0][:], x_d[r0:r0 + 128, :])], [], [xpf[blk][1][0]], f"dma:xpf{blk}")

    def load_block(blk):
        src_t, src_b = xpf[blk]
        s = (blk * 128) // SUB
        for half in range(2):
            bk, bb = psum()
            S.tr([(bk[:, i * 128:(i + 1) * 128], src_t[:, (half * 4 + i) * 128:(half * 4 + i + 1) * 128], identF)
                  for i in range(4)], [src_b[0]] + CB, [bb])
            fn = lambda e, bk=bk, half=half, blk=blk: e.activation(
                out=xT[:, half * 4:half * 4 + 4, blk * 128:(blk + 1) * 128],
                in_=bk[:].rearrange("p (a b) -> p a b", a=4), func=AF.Copy)
            fn2 = lambda e, bk=bk, half=half, blk=blk: e.tensor_copy(
                out=xT[:, half * 4:half * 4 + 4, blk * 128:(blk + 1) * 128],
                in_=bk[:].rearrange("p (a b) -> p a b", a=4))
            wb = [xT_b[half * 4 + i][s] for i in range(4)]
            if half == 0:
                S.op("act", fn, [bb], wb)
            else:
                S.op("dve", fn2, [bb], wb)

    ost_rr = [0]

    def store_block(t, blk):
        r0 = t * T + blk * 128
        s = (blk * 128) // SUB
        for half in range(2):
            oi = ost_rr[0] % NOST
            ost_rr[0] += 1
            ot, ob = ost_o[oi]
            bk, bb = psum()
            S.tr([(bk[:, i * 128:(i + 1) * 128], xT[:, half * 4 + i, blk * 128:(blk + 1) * 128], identF)
                  for i in range(4)], [xT_b[half * 4 + i][s] for i in range(4)] + CB, [bb])
            if half == 0:
                S.op("act", lambda e, bk=bk, ot=ot: e.activation(out=ot[:], in_=bk[:], func=AF.Copy), [bb], [ob[0]])
            else:
                S.op("dve", lambda e, bk=bk, ot=ot: e.tensor_copy(out=ot[:], in_=bk[:]), [bb], [ob[0]])
            S.dma("sp", [(out_d[r0:r0 + 128, half * 512:(half + 1) * 512], ot[:])], [ob[0]], [], f"dma:out{oi}")

    def load_tile(t):
        S.label = "load"
        pcs = norm_pieces(xsrc, xbufs, T, O_G_FFN1, hdst, hdb, 0) if stage >= 1 else None
        for blk in range(T // 128):
            if pcs is not None and blk == 4:
                pcs[0]()
            if pcs is not None and blk == 6:
                pcs[1]()
                pcs[2]()
            load_block(blk)

    def store_tile(t, final_norm=True):
        S.label = "store"
        if final_norm:
            rmsnorm_T(xsrc, xbufs, T, O_G_FIN, xsrc, xbufs, chunks=(1,), defer_last=True)
        for blk in range(T // 128):
            store_block(t, blk)
            if blk == 1:
                run_deferred()

    def boundary(t):
        S.label = "store"
        rmsnorm_T(xsrc, xbufs, T, O_G_FIN, xsrc, xbufs, chunks=(1,), defer_last=True)
        pcs = norm_pieces(xsrc, xbufs, T, O_G_FFN1, hdst, hdb, 0)
        for blk in range(4):
            store_block(t, blk)
            if blk == 1:
                run_deferred()
        for i in range(4):
            S.label = "load"
            load_block(i)
            S.label = "store"
            store_block(t, 4 + i)
        S.label = "load"
        pcs[0]()
        for blk in range(4, 8):
            if blk == 6:
                pcs[1]()
                pcs[2]()
            load_block(blk)

    def proj_groups(wv_cols, nchunk, kch, rhs_fn, rhs_bufs_fn, evac):
        state = {}

        def load():
            st, sbuf = wload([(lambda st: v3(st, kch, nchunk * 128), wv_cols)])
            state["sv"] = v3(st, kch, nchunk * 128)
            state["sbuf"] = sbuf

        groups = []
        for m in range(nchunk):
            for s in range(NSUB):
                def g(m=m, s=s):
                    if "sv" not in state:
                        load()
                    sv, sbuf = state["sv"], state["sbuf"]
                    bk, bb = psum()
                    S.mm([(bk[:], sv[:, kc, m * 128:(m + 1) * 128], rhs_fn(kc, s), kc == 0, kc == kch - 1)
                          for kc in range(kch)], [sbuf] + rhs_bufs_fn(s), [bb])
                    evac(m, s, bk, bb)
                groups.append(g)
        return groups

    def mixer(t):
        S.label = "mix"
        winv = kview(win_d)
        wpre("mix_q", proj_parts(winv[:, :, 0:512], 4, 8))
        wpre("mix_k", proj_parts(winv[:, :, 512:1024], 4, 8))
        norm_x_to_h(O_G_MIX, chunks=(1,))
        S.label = "mix/proj1"
        def ev_qk(si, m, s, bk, bb):
            if si == 0:
                S.op("act", lambda e: e.activation(out=qT[:, m, sl(s)], in_=bk[:], func=AF.Copy, scale=128.0 ** -0.5),
                     [bb], [qT_b[s * 4 + i] for i in range(4)])
            else:
                S.op("dve", lambda e: e.tensor_copy(out=kTt[:, m, sl(s)], in_=bk[:]), [bb],
                     [kT_b[s * 4 + i] for i in range(4)])
        proj_multi([(winv[:, :, 0:512], "mix_q"), (winv[:, :, 512:1024], "mix_k")], 4, 8, hrhs, hrb, ev_qk)
        st, sbuf = wload([(lambda st: v3(st, 8, 16), winv[:, :, 3072:3088])])
        sv = v3(st, 8, 16)
        S.op("dve", lambda e: e.memset(aT[:], 1.0), [], [aT_b[0]])
        for s in range(NSUB):
            bk, bb = psum()
            S.mm([(bk[0:16, :], sv[:, kc, :], hT[:, kc, sl(s)], kc == 0, kc == 7) for kc in range(8)],
                 [sbuf] + hT_b[s], [bb])
            S.op("dve", lambda e, bk=bk, s=s: e.tensor_copy(out=aT[0:16, sl(s)], in_=bk[0:16, :]), [bb], [aT_b[0]])
        vslots = []
        for nh in range(2):
            st, sbuf = wload([(lambda st: v3(st, 8, 512), winv[:, :, 1024 + nh * 512:1024 + (nh + 1) * 512])])
            vslots.append((v3(st, 8, 512), sbuf))

        def v_group(blk, nh):
            def g():
                sv, sbuf = vslots[nh]
                bk, bb = psum()
                S.mm([(bk[:], hT[:, kc, blk * 128:(blk + 1) * 128], sv[:, kc, :], kc == 0, kc == 7) for kc in range(8)],
                     [sbuf] + hT_b[blk // 4], [bb])
                if (blk + nh) % 2 == 0 or blk >= 4:
                    S.op("act", lambda e: e.activation(out=vT[:, blk, nh * 512:(nh + 1) * 512], in_=bk[:],
                                                       func=AF.Copy), [bb], [v_b[blk]])
                else:
                    S.op("dve", lambda e: e.tensor_copy(out=vT[:, blk, nh * 512:(nh + 1) * 512], in_=bk[:]),
                         [bb], [v_b[blk]])
            return g

        r_lists = []
        for half in range(2):
            def ev_r(m, s, bk, bb, half=half):
                if (m + s) % 2 == 0 and s == 0:
                    S.op("dve", lambda e: e.tensor_copy(out=srT[:, half * 4 + m, sl(s)], in_=bk[:]),
                         [bb], [sr_b[s * 4 + i] for i in range(4)])
                else:
                    S.op("act", lambda e: e.activation(out=srT[:, half * 4 + m, sl(s)], in_=bk[:], func=AF.Copy),
                         [bb], [sr_b[s * 4 + i] for i in range(4)])
            r_lists.append(proj_groups(winv[:, :, 2048 + half * 512:2048 + (half + 1) * 512], 4, 8, hrhs, hrb, ev_r))

        def r_groups(s):
            return [r_lists[half][m * 2 + s] for half in range(2) for m in range(4)]

        fillA = [v_group(blk, nh) for blk in range(4) for nh in range(2)] + r_groups(0) + \
                [v_group(blk, nh) for blk in range(4, 8) for nh in range(2)]
        fillB = r_groups(1)

        def fill(lst, n=1):
            for _ in range(n):
                if lst:
                    lab = S.label
                    S.label = "mix/fill"
                    lst.pop(0)()
                    S.label = lab

        fillers = []
        for gi, (tt, tb) in enumerate([(tga, tga_b), (tgb, tgb_b)]):
            base = 3600 + gi * 1024
            for half in range(2):
                def ev_g(m, s, bk, bb, half=half, tt=tt, tb=tb):
                    S.op("act", lambda e: e.activation(out=tt[:, half * 4 + m, sl(s)], in_=bk[:], func=AF.Tanh, scale=0.5),
                         [bb], [tb[(half * 4 + m) * 2 + s]])
                fillers.extend(proj_groups(winv[:, :, base + half * 512:base + (half + 1) * 512], 4, 8, hrhs, hrb, ev_g))

        def filler(n=1):
            for _ in range(n):
                if fillers:
                    lab = S.label
                    S.label = "mix/gates"
                    fillers.pop(0)()
                    S.label = lab

        v4 = lambda ap: ap.rearrange("p (a b) -> p a b", a=4)

        def A1(c):
            p = c % 2
            zb, zb_b = zb2[p]
            lp, lp_b = lp2[p]
            tk = slice(c * 128, (c + 1) * 128)
            bz, bzb = psum()
            S.mm([(bz[:], aT[0:17, tk], walpha[:], True, True)], [aT_b[0]] + CB, [bzb])
            S.op("act", lambda e, bz=bz, zb=zb: e.activation(out=zb[:], in_=bz[:], func=AF.Exp, scale=-1.0),
                 [bzb], [zb_b[0]])
            S.op("act", lambda e, zb=zb, lp=lp: e.activation(out=lp[:], in_=zb[:], func=AF.Ln, bias=1.0),
                 [zb_b[0]], [lp_b[0]])

        def A2(c):
            p = c % 2
            lp, lp_b = lp2[p]
            eb, eb_b = eb2[p]
            enb, enb_b = enb2[p]
            tk = slice(c * 128, (c + 1) * 128)
            bc, bcb = psum()
            S.mm([(bc[:, h * 128:(h + 1) * 128], lp[:, h * 128:(h + 1) * 128], triB[:], True, True) for h in range(4)],
                 [lp_b[0]] + CB, [bcb])
            S.op("act", lambda e, bc=bc, eb=eb: e.activation(out=eb[:], in_=v4(bc[:]), func=AF.Exp, scale=-1.0 / 16.0),
                 [bcb], [eb_b[0]])
            S.op("act", lambda e, bc=bc, enb=enb: e.activation(out=enb[:], in_=v4(bc[:]), func=AF.Exp, scale=1.0 / 16.0),
                 [bcb], [enb_b[0]])
            S.op("dve", lambda e, tk=tk, eb=eb: e.tensor_tensor(out=qT[:, :, tk], in0=qT[:, :, tk], in1=eb[:], op=ALU.mult),
                 [eb_b[0]], [qT_b[c]])
            S.op("dve", lambda e, tk=tk, enb=enb: e.tensor_tensor(out=kTt[:, :, tk], in0=kTt[:, :, tk], in1=enb[:],
                                                                   op=ALU.mult), [enb_b[0]], [kT_b[c]])
            S.op("dve", lambda e, c=c, eb=eb: e.tensor_copy(out=ebl[:, c, :], in_=eb[:, :, 127]), [eb_b[0]], [ebl_b[c]])

        def A3(c):
            tk = slice(c * 128, (c + 1) * 128)
            bs, bsb = psum()
            S.mm([(bs[:, h * 128:(h + 1) * 128], kTt[:, h, tk], qT[:, h, tk], True, True) for h in range(4)],
                 [kT_b[c], qT_b[c]], [bsb])
            bt, btb = psum()
            btv = bt[:].bitcast(BF16)
            S.tr([(btv[:, h * 128:(h + 1) * 128], kTt[:, h, tk], identB[:]) for h in range(4)], [kT_b[c]] + CB, [btb])
            S.op("dve", lambda e, bs=bs, c=c: e.tensor_tensor(
                out=v4(smka[:, c, :]), in0=v4(bs[:]), in1=triB[:].unsqueeze(1).to_broadcast([128, 4, 128]),
                op=ALU.mult), [bsb] + CB, [smka_b[c]])
            S.op("act", lambda e, btv=btv, c=c: e.activation(out=kgTa[:, c, :], in_=btv[:, 0:512], func=AF.Copy),
                 [btb], [kgTa_b[c]])

        def B_kv(c):
            bbs = []
            for hh in range(2):
                bk, bb = psum_at(4 + hh)
                S.mm([(bk[:, h2 * 256:(h2 + 1) * 256], kgTa[:, c, (hh * 2 + h2) * 128:(hh * 2 + h2 + 1) * 128],
                       vT[:, c, (hh * 2 + h2) * 256:(hh * 2 + h2 + 1) * 256], True, True) for h2 in range(2)],
                     [kgTa_b[c], v_b[c]], [bb])
                bbs.append(bb)
            return bbs

        def B_o(c):
            tk = slice(c * 128, (c + 1) * 128)
            Sp = Sbf2[(c - 1) % 2]
            bbs = []
            for hh in range(2):
                bk, bb = psum_at((c % 2) * 2 + hh)
                items = []
                for h2 in range(2):
                    h = hh * 2 + h2
                    for vc in range(2):
                        col = (h2 * 2 + vc) * 128
                        items.append((bk[:, col:col + 128], vT[:, c, h * 256 + vc * 128:h * 256 + (vc + 1) * 128],
                                      smka[:, c, h * 128:(h + 1) * 128], True, False))
                        items.append((bk[:, col:col + 128], Sp[:, h, vc * 128:(vc + 1) * 128], qT[:, h, tk], False, True))
                S.mm(items, [v_b[c], smka_b[c]] + Sbf_b[(c - 1) % 2] + [qT_b[c]], [bb])
                bbs.append(bb)
            return bbs

        def o_view(c):
            b0 = (c % 2) * 2
            return psum_all[:, b0 * 512:(b0 + 2) * 512].rearrange("p (a b) -> p a b", a=8)

        def B_state(c, kvb):
            Sn = Sbf2[c % 2]
            kv = psum_all[:, 4 * 512:6 * 512].rearrange("p (a b) -> p a b", a=4)
            ebc = ebl[:, c, :].unsqueeze(2).to_broadcast([128, 4, 256])
            S.op("dve", lambda e: e.tensor_tensor(out=S32[:], in0=kv, in1=S32[:], op=ALU.add), kvb, S32_b)
            S.op("dve", lambda e: e.tensor_tensor(out=Sn[:], in0=S32[:], in1=ebc, op=ALU.mult),
                 [ebl_b[c]] + S32_b, Sbf_b[c % 2])
            S.op("dve", lambda e: e.tensor_tensor(out=S32[:], in0=S32[:], in1=ebc, op=ALU.mult), [ebl_b[c]], S32_b)

        def B_sq(c, ob):
            S.op("act", lambda e: e.activation(out=osq[:], in_=o_view(c), func=AF.Square), ob, [osq_b[0]])

        def B_ones(c):
            bn, bnb = psum_at(6)
            S.mm([(bn[:, h * 128:(h + 1) * 128], onesV[:], osq[:, h * 2 + vc, :], vc == 0, vc == 1)
                  for h in range(4) for vc in range(2)], [osq_b[0]] + CB, [bnb])
            return bn, bnb

        def B_lnexp(c, bnp):
            bn, bnb = bnp
            S.op("act", lambda e, bn=bn: e.activation(out=rsn[:], in_=v4(bn[:]), func=AF.Ln, bias=EPS), [bnb], [rsn_b[0]])
            S.op("act", lambda e: e.activation(out=rsn[:], in_=rsn[:], func=AF.Exp, scale=-0.5), [], [rsn_b[0]])

        def B_norm(c, ob):
            tk = slice(c * 128, (c + 1) * 128)
            ov_ = o_view(c).rearrange("p (h v) i -> p h v i", v=2)
            og_ = ogT[:, :, tk].rearrange("p (h v) i -> p h v i", v=2)
            for vc in range(2):
                S.op("dve", lambda e, vc=vc: e.scalar_tensor_tensor(
                    out=og_[:, :, vc, :], in0=ov_[:, :, vc, :], scalar=small[:, O_GHEAD + vc:O_GHEAD + vc + 1],
                    in1=rsn[:], op0=ALU.mult, op1=ALU.mult), ob + [rsn_b[0]] + CB, [og_b[c][vc]])

        S.label = "mix/glaA"
        for step in range(8 + 2):
            if step < 8:
                A1(step)
                fill(fillA)
            if 0 <= step - 1 < 8:
                A2(step - 1)
                fill(fillA)
            if 0 <= step - 2 < 8:
                A3(step - 2)
                fill(fillA)
        fill(fillA, len(fillA))
        S.label = "mix/glaB"
        psum_only7[0] = True
        prev = None
        for c in range(8):
            kvb = B_kv(c)
            B_state(c, kvb)
            ob = B_o(c)
            fill(fillB)
            if prev is not None:
                B_sq(*prev)
                bnp = B_ones(prev[0])
                B_lnexp(prev[0], bnp)
                B_norm(*prev)
            prev = (c, ob)
        fill(fillB, len(fillB))
        B_sq(*prev)
        bnp = B_ones(prev[0])
        B_lnexp(prev[0], bnp)
        B_norm(*prev)
        psum_only7[0] = False
        for s_ in range(NSUB):
            S.op("act", lambda e, s_=s_: e.activation(out=srT[:, :, sl(s_)], in_=srT[:, :, sl(s_)], func=AF.Silu),
                 [], [sr_b[s_ * 4 + i] for i in range(4)])
            S.op("dve", lambda e, s_=s_: e.tensor_tensor(out=ogT[:, :, sl(s_)], in0=ogT[:, :, sl(s_)],
                                                         in1=srT[:, :, sl(s_)], op=ALU.mult),
                 [sr_b[s_ * 4 + i] for i in range(4)], [b_ for i in range(4) for b_ in og_b[s_ * 4 + i]])

        S.label = "mix/pool"
        if t > 0:
            S.op("dve", lambda e: e.tensor_copy(out=uT[:, :, 0:16], in_=uhalo[:]), [uhalo_b], [uT_b[0]])
        else:
            S.op("dve", lambda e: e.memset(uT[:, :, 0:16], 0.0), [], [uT_b[0]])
        def ev_u(m, s, bk, bb):
            S.op("act", lambda e: e.activation(out=uT[:, m, 16 + s * SUB:16 + (s + 1) * SUB], in_=bk[:], func=AF.Copy),
                 [bb], [uT_b[0]])
        proj_T(winv[:, :, 3088:3600], 4, 8, hrhs, hrb, ev_u)
        S.op("dve", lambda e: e.tensor_copy(out=uhalo[:], in_=uT[:, :, T:T + 16]), [uT_b[0]], [uhalo_b])
        W = 16 + SUB
        for s in range(NSUB):
            for g in range(4):
                usrc = uT[:, g, s * SUB:s * SUB + W]
                cur_t, cur_b = None, None
                bufs = [(pwA, pwA_b[0]), (pwB, pwB_b[0])]
                for it in range(g + 1):
                    sh = 1 << it
                    v0 = 2 * sh - 1
                    dt_, db_ = bufs[it % 2]
                    if it == 0:
                        S.op("dve", lambda e, dt_=dt_, usrc=usrc, sh=sh, v0=v0: e.tensor_tensor(
                            out=dt_[:, v0:W], in0=usrc[:, v0:W], in1=usrc[:, v0 - sh:W - sh], op=ALU.add), [uT_b[0]], [db_])
                    else:
                        S.op("dve", lambda e, dt_=dt_, ct=cur_t, sh=sh, v0=v0: e.tensor_tensor(
                            out=dt_[:, v0:W], in0=ct[:, v0:W], in1=ct[:, v0 - sh:W - sh], op=ALU.add), [cur_b], [db_])
                    cur_t, cur_b = dt_, db_
                wdw = 2 << g
                S.op("dve", lambda e, ct=cur_t, usrc=usrc, g=g, s=s, wdw=wdw: e.scalar_tensor_tensor(
                    out=dfT[:, g, sl(s)], in0=ct[:, 16:W], scalar=1.0 / wdw, in1=usrc[:, 16:W],
                    op0=ALU.mult, op1=ALU.subtract), [cur_b, uT_b[0]], [df_b[g * 2 + s]])
                if t == 0 and s == 0:
                    S.op("dve", lambda e, ct=cur_t, g=g: e.tensor_tensor(
                        out=ct[:, 16:32], in0=ct[:, 16:32], in1=small[:, O_CINV + g * 16:O_CINV + (g + 1) * 16],
                        op=ALU.mult), [df_b[g * 2 + s]] + CB, [cur_b])
                    S.op("dve", lambda e, ct=cur_t, usrc=usrc, g=g: e.tensor_tensor(
                        out=dfT[:, g, 0:16], in0=ct[:, 16:32], in1=usrc[:, 16:32], op=ALU.subtract),
                        [cur_b, uT_b[0]], [df_b[g * 2 + s]])
        S.label = "mix/gates"
        filler(len(fillers))
        S.label = "mix/pool"
        st, sbuf = wload([(lambda st: v3(st, 4, 128), pmix_d.rearrange("g c d -> c g d"))])
        sv = v3(st, 4, 128)
        for g in range(4):
            for s in range(NSUB):
                bk, bb = psum()
                S.mm([(bk[:], sv[:, g, :], dfT[:, g, sl(s)], True, True)], [sbuf, df_b[g * 2 + s]], [bb])
                S.op("act", lambda e, bk=bk, g=g, s=s: e.activation(
                    out=zT[:, g, sl(s)], in_=bk[:], func=AF.Copy, scale=small[:, O_PSCALE + g:O_PSCALE + g + 1]),
                    [bb] + CB, [zT_b[g * 2 + s]])
        S.label = "mix/merge"
        wav = kview(wupa_d)
        wbv = kview(wupb_d)
        stb, sbufb = wload([(lambda st: v3(st, 4, 1024), wbv)])
        svb = v3(stb, 4, 1024)
        for half in range(2):
            sta, sbufa = wload([(lambda st: v3(st, 8, 512), wav[:, :, half * 512:(half + 1) * 512])])
            sva = v3(sta, 8, 512)
            for mm_ in range(4):
                m = half * 4 + mm_
                for s in range(NSUB):
                    ba, bab = psum()
                    S.mm([(ba[:], sva[:, kc, mm_ * 128:(mm_ + 1) * 128], ogT[:, kc, sl(s)], kc == 0, kc == 7)
                          for kc in range(8)], [sbufa] + [b_ for i in range(4) for b_ in og_b[s * 4 + i]], [bab])
                    bbk, bbb = psum()
                    S.mm([(bbk[:], svb[:, g, m * 128:(m + 1) * 128], zT[:, g, sl(s)], g == 0, g == 3)
                          for g in range(4)], [sbufb] + [zT_b[g * 2 + s] for g in range(4)], [bbb])
                    gbuf = tga_b[m * 2 + s]
                    S.op("dve", lambda e, ba=ba, m=m, s=s: e.scalar_tensor_tensor(
                        out=t1[:], in0=tga[:, m, sl(s)], scalar=1.0, in1=ba[:], op0=ALU.add, op1=ALU.mult),
                        [bab, gbuf], [t1_b[0]])
                    S.op("dve", lambda e, bbk=bbk, m=m, s=s: e.scalar_tensor_tensor(
                        out=tga[:, m, sl(s)], in0=tgb[:, m, sl(s)], scalar=1.0, in1=bbk[:], op0=ALU.add, op1=ALU.mult),
                        [bbb, tgb_b[m * 2 + s]], [gbuf])
                    S.op("dve", lambda e, m=m, s=s: e.tensor_tensor(
                        out=tga[:, m, sl(s)], in0=tga[:, m, sl(s)], in1=t1[:], op=ALU.add), [t1_b[0]], [gbuf])
        S.label = "mix/out"
        wmv = kview(wmix_d)

        def ev_o(si, m, s, bk, bb):
            mm_ = si * 4 + m
            S.op("dve", lambda e: e.scalar_tensor_tensor(
                out=xT[:, mm_, sl(s)], in0=bk[:], scalar=0.5, in1=xT[:, mm_, sl(s)], op0=ALU.mult, op1=ALU.add),
                [bb], [xT_b[mm_][s]])
        proj_multi([(wmv[:, :, half * 512:(half + 1) * 512], None) for half in range(2)], 4, 8,
                   lambda kc, s: tga[:, kc, sl(s)], lambda s: [tga_b[kc * 2 + s] for kc in range(8)], ev_o,
                   hook=(tail_hook(O_G_XA, 8) if stage >= 3 else None))

    def xattn():
        S.label = "xa"
        wqv = kview(wq_d)
        for half in range(2):
            wpre(("xa_q", half), proj_parts(wqv[:, :, half * 512:(half + 1) * 512], 4, 8))
        norm_x_to_h(O_G_XA, chunks=(1,))
        S.label = "xa/q"
        def ev_q(si, m, s, bk, bb):
            c = si * 4 + m
            if (m + s) % 2 == 0:
                S.op("act", lambda e: e.activation(out=xq[:, c, sl(s)], in_=bk[:], func=AF.Copy), [bb], [xq_b[c * 2 + s]])
            else:
                S.op("dve", lambda e: e.tensor_copy(out=xq[:, c, sl(s)], in_=bk[:]), [bb], [xq_b[c * 2 + s]])
        proj_multi([(wqv[:, :, half * 512:(half + 1) * 512], ("xa_q", half)) for half in range(2)], 4, 8, hrhs, hrb, ev_q)
        S.label = "xa/attn"
        its = [(h, s_) for h in range(4) for s_ in range(NSUB)]

        def scores(i):
            h, s_ = its[i]
            pms = []
            for mb in range(2):
                bk, bb = psum()
                S.mm([(bk[:], KT[:, h * 2 + dc, mb * 128:(mb + 1) * 128], xq[:, h * 2 + dc, sl(s_)], dc == 0, dc == 1)
                      for dc in range(2)], [KT_b, xq_b[(h * 2) * 2 + s_], xq_b[(h * 2 + 1) * 2 + s_]], [bb])
                pt, pb = Pm[(i % 2) * 2 + mb]
                S.op("act", lambda e, bk=bk, pt=pt: e.activation(out=pt[:], in_=bk[:], func=AF.Exp, scale=1.0 / 16.0),
                     [bb], [pb[0]])
                pms.append((pt, pb[0]))
            return pms

        nxt = scores(0)
        for i, (h, s_) in enumerate(its):
            pms = nxt
            if i + 1 < len(its):
                nxt = scores(i + 1)
            rz, rz_b = rz2[i % 2]
            bz, bzb = psum()
            S.mm([(bz[:], ones1[:], pms[mb][0][:], mb == 0, mb == 1) for mb in range(2)],
                 [pms[0][1], pms[1][1]] + CB, [bzb])
            S.op("dve", lambda e, bz=bz, rz=rz: e.reciprocal(out=rz[:], in_=bz[:]), [bzb], [rz_b[0]])
            for dvc in range(2):
                bk, bb = psum()
                S.mm([(bk[:], Vm[:, mb, h * 256 + dvc * 128:h * 256 + (dvc + 1) * 128], pms[mb][0][:], mb == 0, mb == 1)
                      for mb in range(2)], [Vm_b, pms[0][1], pms[1][1]], [bb])
                c = h * 2 + dvc
                S.op("dve", lambda e, bk=bk, c=c, s_=s_, rz=rz: e.tensor_tensor(
                    out=xo[:, c, sl(s_)], in0=bk[:], in1=rz[:], op=ALU.mult), [bb, rz_b[0]], [xo_b[c * 2 + s_]])
        S.label = "xa/o"
        wov = kview(wo_d)

        def ev_o(si, m, s, bk, bb):
            mm_ = si * 4 + m
            S.op("dve", lambda e: e.tensor_tensor(out=xT[:, mm_, sl(s)], in0=bk[:], in1=xT[:, mm_, sl(s)], op=ALU.add),
                 [bb], [xT_b[mm_][s]])
        proj_multi([(wov[:, :, half * 512:(half + 1) * 512], None) for half in range(2)], 4, 8,
                   lambda kc, s: xo[:, kc, sl(s)], lambda s: [xo_b[kc * 2 + s] for kc in range(8)], ev_o,
                   hook=(tail_hook(O_G_FFN2, 8) if stage >= 4 else None))

    mem_prologue()
    prefetch_tile(0)
    load_tile(0)
    for t in range(nt):
        if stage >= 1:
            S.label = "ffn0"
            ffn_pre(0)
            norm_x_to_h(O_G_FFN1, chunks=(1,))
            ffn(0, norm_pieces(xsrc, xbufs, T, O_G_MIX, hdst, hdb, 0) if stage >= 2 else None)
        if stage >= 2:
            mixer(t)
        if stage >= 3:
            xattn()
        if t + 1 < nt:
            prefetch_tile(t + 1)
        if stage >= 4:
            S.label = "ffn1"
            norm_x_to_h(O_G_FFN2, chunks=(1,))
            ffn(1, norm_pieces(xsrc, xbufs, T, O_G_FIN, xsrc, xbufs, 0) if stage >= 5 else None)
        if t + 1 < nt and stage >= 5:
            boundary(t)
        else:
            store_tile(t, final_norm=(stage >= 5))
            if t + 1 < nt:
                load_tile(t + 1)
    fin = []
    for key in [f"dma:out{i}" for i in range(8)]:
        if key in S.dmacnt:
            b = Buf(key)
            b.w = (key, S.dmacnt[key])
            fin.append(b)
    S.wait_all("sp", fin)
    S.emit()
    nc._pe_labels = S.pe_labels
    return nc


def make_small(inp):
    small = np.zeros((128, NS), np.float32)

    def gv(a):
        return np.ascontiguousarray(np.asarray(a, np.float32).reshape(-1, 128).T)

    small[:, O_G_FFN1:O_G_FFN1 + 8] = gv(inp["ffn1_norm"])
    small[:, O_G_MIX:O_G_MIX + 8] = gv(inp["mix_norm"])
    small[:, O_G_XA:O_G_XA + 8] = gv(inp["xa_norm"])
    small[:, O_G_MEM:O_G_MEM + 8] = gv(inp["mem_norm"])
    small[:, O_G_FFN2:O_G_FFN2 + 8] = gv(inp["ffn2_norm"])
    small[:, O_G_FIN:O_G_FIN + 8] = gv(inp["final_norm"])
    small[:, O_GHEAD:O_GHEAD + 2] = gv(inp["gla_head_norm"])
    small[:, O_PSCALE:O_PSCALE + 4] = gv(inp["pool_scale"])
    for g in range(4):
        w = 2 << g
        small[:, O_CINV + g * 16:O_CINV + (g + 1) * 16] = (1.0 / np.minimum(np.arange(1, 17), w))[None, :]
    small[:, O_BALPHA:O_BALPHA + 512] = np.asarray(inp["b_alpha"], np.float32).reshape(1, 512)
    small[:, O_IDENT:O_IDENT + 128] = np.eye(128, dtype=np.float32)
    small[:, O_TRI:O_TRI + 128] = np.triu(np.ones((128, 128), np.float32))
    return small


_NC_CACHE = {}


def kernel(**inputs):
    inp = {k: np.asarray(v) for k, v in inputs.items()}
    key = "full"
    if key not in _NC_CACHE:
        _NC_CACHE[key] = build_program()
    nc = _NC_CACHE[key]
    shared = {"small": make_small(inp), "b_alpha": np.ascontiguousarray(inp["b_alpha"], dtype=np.float32).reshape(1, 512)}
    for name in ("ffn1_w1", "ffn1_w3", "ffn1_w2", "ffn2_w1", "ffn2_w3", "ffn2_w2", "w_in", "w_alpha", "w_up_a",
                 "pool_mix", "w_up_b", "w_mix_out", "xa_wq", "xa_wk", "xa_wv", "xa_wo"):
        shared[name] = np.ascontiguousarray(inp[name][0], dtype=np.float32)
    in_maps = []
    for b in range(8):
        m = dict(shared)
        m["x"] = np.ascontiguousarray(inp["x"][b], dtype=np.float32)
        m["mem"] = np.ascontiguousarray(inp["mem"][b], dtype=np.float32)
        in_maps.append(m)
    res = run_bass_kernel_spmd(nc, in_maps, core_ids=list(range(8)))
    out = np.stack([np.asarray(r["out"], dtype=np.float32) for r in res.results], axis=0)
    return out
```

```python
import numpy as np
import concourse.bass as bass
import concourse.mybir as mybir
from concourse.bass_utils import run_bass_kernel_spmd

F32 = mybir.dt.float32
BF16 = mybir.dt.bfloat16
AF = mybir.ActivationFunctionType
ALU = mybir.AluOpType

D = 1024
SEQ = 4096
T = 1024
SUB = 512
NSUB = T // SUB
DFF = 2816
NJ = DFF // 128
MEM = 256
IN_TOTAL = 5648
EPS = 1e-6
NSLOT = 5
SLOT_ELEMS = 4096

O_G_FFN1, O_G_MIX, O_G_XA, O_G_MEM, O_G_FFN2, O_G_FIN = 0, 8, 16, 24, 32, 40
O_GHEAD, O_PSCALE, O_CINV, O_BALPHA, O_IDENT, O_TRI = 48, 50, 54, 118, 630, 758
NS = 886


class Buf:
    __slots__ = ("name", "w", "r", "alias")

    def __init__(self, name):
        self.name = name
        self.w = None
        self.r = {}
        self.alias = []


class Sched:
    def __init__(self, nc):
        self.nc = nc
        self.names = ["pe", "act", "dve", "pool", "sp"]
        self.semh = {k: nc.alloc_semaphore(name="sem_" + k) for k in self.names}
        self.cnt = {k: 0 for k in self.names}
        self.known = {k: {} for k in self.names}
        self.prog = {k: [] for k in self.names}
        self.dmacnt = {}
        self.label = ""
        self.pe_labels = []

    def dma_sem(self, key):
        if key not in self.semh:
            self.semh[key] = self.nc.alloc_semaphore(name="sem_" + key.replace(":", "_"))
            self.dmacnt[key] = 0
        return key

    def _deps(self, eng, reads, writes):
        deps = {}

        def add(kv):
            if kv is None:
                return
            k, v = kv
            if deps.get(k, 0) < v:
                deps[k] = v

        for b in reads:
            add(b.w)
        for b in writes:
            add(b.w)
            for k, v in b.r.items():
                add((k, v))
            for a in b.alias:
                add(a.w)
                for k, v in a.r.items():
                    add((k, v))
        for k, v in deps.items():
            if k == "pe" and eng == "pe":
                continue
            if self.known[eng].get(k, 0) < v:
                self.known[eng][k] = v
                self.prog[eng].append(("wait", k, v))

    def op(self, eng, fn, reads=(), writes=()):
        self._deps(eng, reads, writes)
        self.cnt[eng] += 1
        val = self.cnt[eng]
        self.prog[eng].append(("ins", fn, True))
        for b in reads:
            b.r[eng] = val
        for b in writes:
            b.w = (eng, val)
            b.r = {}
        return val

    def mm(self, items, reads, writes):
        eng = "pe"
        self._deps(eng, reads, writes)
        self.cnt[eng] += 1
        val = self.cnt[eng]
        n = len(items)
        self.pe_labels.extend([self.label] * n)
        for i, (o, l, r, st, sp) in enumerate(items):
            self.prog[eng].append(
                ("ins", (lambda e, o=o, l=l, r=r, st=st, sp=sp: e.matmul(o, l, r, start=st, stop=sp)), i == n - 1)
            )
        for b in reads:
            b.r[eng] = val
        for b in writes:
            b.w = (eng, val)
            b.r = {}

    def tr(self, items, reads, writes):
        eng = "pe"
        self._deps(eng, reads, writes)
        self.cnt[eng] += 1
        val = self.cnt[eng]
        n = len(items)
        self.pe_labels.extend([self.label] * n)
        for i, (o, a, idn) in enumerate(items):
            self.prog[eng].append(("ins", (lambda e, o=o, a=a, idn=idn: e.transpose(o, a, idn)), i == n - 1))
        for b in reads:
            b.r[eng] = val
        for b in writes:
            b.w = (eng, val)
            b.r = {}

    def dma(self, queue, pairs, reads, writes, key):
        self.dma_sem(key)
        self._deps(queue, reads, writes)
        for (o, i) in pairs:
            self.dmacnt[key] += 16
            self.prog[queue].append(("dma", (lambda e, o=o, i=i: e.dma_start(out=o, in_=i)), key))
        val = self.dmacnt[key]
        for b in reads:
            b.r[key] = val
        for b in writes:
            b.w = (key, val)
            b.r = {}

    def wait_all(self, eng, bufs):
        self._deps(eng, bufs, bufs)

    def emit(self):
        nc = self.nc
        engs = {"pe": "tensor", "act": "scalar", "dve": "vector", "pool": "gpsimd", "sp": "sync"}
        with nc.Block() as block:
            for name in self.names:
                prog = self.prog[name]
                own = self.semh[name]

                def body(e, prog=prog, own=own):
                    for item in prog:
                        if item[0] == "wait":
                            e.wait_ge(self.semh[item[1]], item[2])
                        elif item[0] == "ins":
                            ins = item[1](e)
                            if item[2]:
                                ins.then_inc(own, 1)
                        else:
                            ins = item[1](e)
                            ins.then_inc(self.semh[item[2]], 16)

                getattr(block, engs[name])(body)


def build_program(nt=SEQ // T, stage=99):
    nc = bass.Bass("TRN2", target_bir_lowering=False)
    S = Sched(nc)

    def din(name, shape):
        return nc.dram_tensor(name, list(shape), F32, kind="ExternalInput").ap()

    x_d = din("x", [SEQ, D])
    mem_d = din("mem", [MEM, D])
    small_d = din("small", [128, NS])
    w1_d = [din("ffn1_w1", [D, DFF]), din("ffn2_w1", [D, DFF])]
    w3_d = [din("ffn1_w3", [D, DFF]), din("ffn2_w3", [D, DFF])]
    w2_d = [din("ffn1_w2", [DFF, D]), din("ffn2_w2", [DFF, D])]
    win_d = din("w_in", [D, IN_TOTAL])
    walpha_d = din("w_alpha", [16, 512])
    balpha_d = din("b_alpha", [1, 512])
    wupa_d = din("w_up_a", [D, D])
    pmix_d = din("pool_mix", [4, 128, 128])
    wupb_d = din("w_up_b", [512, D])
    wmix_d = din("w_mix_out", [D, D])
    wq_d = din("xa_wq", [D, D])
    wk_d = din("xa_wk", [D, D])
    wv_d = din("xa_wv", [D, D])
    wo_d = din("xa_wo", [D, D])
    out_d = nc.dram_tensor("out", [SEQ, D], F32, kind="ExternalOutput").ap()

    def kview(w):
        return w.rearrange("(kc p) n -> p kc n", p=128)

    def sb(name, shape, dt):
        return nc.alloc_sbuf_tensor("sb_" + name, list(shape), dt)

    xT = sb("xT", [128, 8, T], F32)
    hT = sb("hT", [128, 8, T], BF16)
    slots = [sb(f"slot{i}", [128, SLOT_ELEMS], BF16) for i in range(NSLOT)]
    small = sb("small", [128, NS], F32)
    KT = sb("KT", [128, 8, MEM], BF16)
    Vm = sb("Vm", [128, 2, D], BF16)
    S32 = sb("S32", [128, 4, 256], F32)
    Sbf2 = [sb(f"Sbf{i}", [128, 4, 256], BF16) for i in range(2)]
    identB = sb("identB", [128, 128], BF16)
    triB = sb("triB", [128, 128], BF16)
    onesD = sb("onesD", [128, 128], BF16)
    onesV = sb("onesV", [128, 128], BF16)
    ones1 = sb("ones1", [128, 128], BF16)
    walpha = sb("walpha", [17, 512], BF16)
    uhalo = sb("uhalo", [128, 4, 16], F32)
    uhalo_b = Buf("uhalo")
    ARENA = 97 * 1024
    arena = sb("arena", [128, ARENA // 4], F32)
    abase = nc.lookup_mloc(arena).addr

    overlay = []

    def ov(name, off_k, shape, dt, nb=1):
        esz = 4 if dt == F32 else 2
        size = int(np.prod(shape[1:])) * esz
        lo = int(off_k * 1024)
        hi = lo + size
        assert hi <= ARENA, (name, hi)
        t = nc.alloc_sbuf_tensor_at("ov_" + name, list(shape), dt, offset=abase + lo)
        bufs = [Buf(f"{name}{i}") for i in range(nb)]
        for (l2, h2, b2) in overlay:
            if l2 < hi and lo < h2:
                for b in bufs:
                    for o in b2:
                        b.alias.append(o)
                        o.alias.append(b)
        overlay.append((lo, hi, bufs))
        return t, bufs

    stg_o = [ov(f"stg{i}", 16 + 4 * i, [128, D], F32) for i in range(2)]
    stg = [x[0] for x in stg_o]
    NOST = 8
    ost_o = [ov(f"ost{i}", 2 * i, [128, 512], F32) for i in range(NOST)]
    gated, gated_b = ov("gated", 0, [128, NJ, T], BF16, nb=NJ * NSUB)
    sl_t = [ov(f"sl{i}", 44 + i, [128, SUB], BF16) for i in range(2)]
    sq, sq_b = ov("sq", 46, [128, 8, SUB], BF16)
    lnv, lnv_b = ov("lnv", 54, [128, SUB], F32)
    rstd, rstd_b = ov("rstd", 56, [128, SUB], F32)
    ptmp, ptmp_b = ov("ptmp", 36, [128, 4, SUB], F32)
    xpf = [ov(f"xpf{i}", 58 + 4 * i, [128, D], F32) for i in range(8)]
    qT, qT_b = ov("qT", 0, [128, 4, T], BF16, nb=8)
    kTt, kT_b = ov("kT", 8, [128, 4, T], BF16, nb=8)
    vT, v_b = ov("vtok", 16, [128, 8, D], BF16, nb=8)
    srT, sr_b = ov("srT", 32, [128, 8, T], BF16, nb=8)
    aT, aT_b = ov("aT", 48, [32, T], BF16, nb=1)
    zb2 = [ov(f"zb{i}", 50 + 2 * i, [128, 512], F32) for i in range(2)]
    lp2 = [ov(f"lp{i}", 54 + i, [128, 512], BF16) for i in range(2)]
    enb2 = [ov(f"enb{i}", 56 + 2 * i, [128, 4, 128], F32) for i in range(2)]
    eb2 = [ov(f"eb{i}", 60 + 2 * i, [128, 4, 128], F32) for i in range(2)]
    kgTa, kgTa_b = ov("kgTa", 64, [128, 8, 512], BF16, nb=8)
    smka, smka_b = ov("smka", 72, [128, 8, 512], BF16, nb=8)
    ebl, ebl_b = ov("ebl", 80, [128, 8, 4], F32, nb=8)
    osq, osq_b = ov("osq", 50, [128, 8, 128], BF16)
    rsn, rsn_b = ov("rsn", 52, [128, 4, 128], F32)
    onT, onT_b = ov("onT", 54, [128, 8, 128], BF16)
    ogT, og_b2 = ov("ogT", 81, [128, 8, T], BF16, nb=16)
    og_b = [[og_b2[2 * c], og_b2[2 * c + 1]] for c in range(8)]
    tga, tga_b = ov("tga", 0, [128, 8, T], BF16, nb=16)
    tgb, tgb_b = ov("tgb", 16, [128, 8, T], BF16, nb=16)
    uT, uT_b = ov("uT", 32, [128, 4, 16 + T], F32, nb=1)
    pwA, pwA_b = ov("pwA", 48.25, [128, 16 + SUB], F32)
    pwB, pwB_b = ov("pwB", 50.5, [128, 16 + SUB], F32)
    dfT, df_b = ov("dfT", 52.75, [128, 4, T], BF16, nb=8)
    zT, zT_b = ov("zT", 60.75, [128, 4, T], BF16, nb=8)
    t1, t1_b = ov("t1", 69, [128, SUB], F32)
    xq, xq_b = ov("xq", 0, [128, 8, T], BF16, nb=16)
    xo, xo_b = ov("xo", 16, [128, 8, T], BF16, nb=16)
    Pm = [ov(f"Pm{i}", 32 + i, [128, SUB], BF16) for i in range(4)]
    rz2 = [ov(f"rz{i}", 36 + 2 * i, [128, SUB], F32) for i in range(2)]
    memT, memT_b = ov("memT", 0, [128, 8, MEM], F32)
    mhT, mhT_b = ov("mhT", 8, [128, 8, MEM], BF16)

    psum_all = nc.alloc_psum_tensor("psum_all", [128, 8 * 512], F32)
    banks = [psum_all[:, i * 512:(i + 1) * 512] for i in range(8)]
    bank_b = [Buf(f"bank{i}") for i in range(8)]
    bank_rr = [0]

    psum_only7 = [False]

    def psum():
        if psum_only7[0]:
            return banks[7], bank_b[7]
        i = bank_rr[0] % 8
        bank_rr[0] += 1
        return banks[i], bank_b[i]

    def psum_at(i):
        return banks[i], bank_b[i]

    xT_b = [[Buf(f"xT{m}_{s}") for s in range(NSUB)] for m in range(8)]
    hT_b = [[Buf(f"hT{s}_{kc}") for kc in range(8)] for s in range(NSUB)]
    slot_b = [Buf(f"slot{i}") for i in range(NSLOT)]
    stg_b = [x[1][0] for x in stg_o]
    const_b = Buf("const")
    KT_b, Vm_b = Buf("KT"), Buf("Vm")
    S32_b = [Buf("S32a"), Buf("S32b")]
    Sbf_b = [[Buf(f"Sbf{p}{hh}") for hh in range(2)] for p in range(2)]
    slot_rr = [0]

    pre_issued = {}

    def wload(parts, key=None, slot=None):
        if key is not None and key in pre_issued:
            return pre_issued.pop(key)
        if slot is None:
            i = slot_rr[0] % NSLOT
            slot_rr[0] += 1
        else:
            i = slot
        st = slots[i]
        S.dma("pool", [(vf(st), src) for (vf, src) in parts], [], [slot_b[i]], f"dma:slot{i}")
        last_slot[0] = i
        return st, slot_b[i]

    last_slot = [0]

    def wpre(key, parts):
        pre_issued[key] = wload(parts)

    def v3(st, a, b):
        return st[:, 0:a * b].rearrange("p (a b) -> p a b", a=a)

    S.dma("sp", [(small[:], small_d)], [], [const_b], "dma:const")
    wa_b = Buf("walpha")
    S.dma("pool", [(walpha[0:16, :], walpha_d), (walpha[16:17, :], balpha_d)], [], [wa_b], "dma:const2")
    cb2 = Buf("const2")
    S.op("dve", lambda e: e.tensor_copy(out=identB[:], in_=small[:, O_IDENT:O_IDENT + 128]), [const_b], [cb2])
    S.op("dve", lambda e: e.tensor_copy(out=triB[:], in_=small[:, O_TRI:O_TRI + 128]), [const_b], [cb2])
    S.op("dve", lambda e: e.memset(onesD[:], 1.0 / 1024.0), [], [cb2])
    S.op("dve", lambda e: e.memset(onesV[:], 1.0 / 256.0), [], [cb2])
    S.op("dve", lambda e: e.memset(ones1[:], 1.0), [], [cb2])
    S.op("dve", lambda e: e.memset(S32[:], 0.0), [], S32_b)
    for p_ in range(2):
        S.op("dve", lambda e, p_=p_: e.memset(Sbf2[p_][:], 0.0), [], Sbf_b[p_])
    CB = [const_b, cb2, wa_b]
    identF = small[:, O_IDENT:O_IDENT + 128]

    def gvec(off, kc):
        return small[:, off + kc:off + kc + 1]

    sq_b2 = [Buf("sqA"), Buf("sqB")]
    for b_ in sq_b2:
        b_.alias = sq_b[0].alias

    def norm_pieces(src3, src_bufs_fn, ntok, goff, dst3, dst_bufs_fn, c):
        lo, hi = c * SUB, min(ntok, (c + 1) * SUB)
        w = hi - lo
        sb_ = src_bufs_fn(c)
        st = {}

        def p_sq():
            for hf in range(2):
                S.op("act", lambda e, hf=hf: e.activation(
                    out=sq[:, hf * 4:hf * 4 + 4, 0:w], in_=src3(lo, hi)[:, hf * 4:hf * 4 + 4, :], func=AF.Square),
                    sb_, [sq_b2[hf]])

        def p_ones():
            bk, bb = psum()
            st["bk"], st["bb"] = bk, bb
            S.mm([(bk[:, 0:w], onesD[:], sq[:, kc, 0:w], kc == 0, kc == 7) for kc in range(8)],
                 sq_b2 + CB, [bb])

        def p_rest():
            bk, bb = st["bk"], st["bb"]
            S.op("act", lambda e: e.activation(out=lnv[:, 0:w], in_=bk[:, 0:w], func=AF.Ln, bias=EPS),
                 [bb], [lnv_b[0]])
            S.op("act", lambda e: e.activation(out=rstd[:, 0:w], in_=lnv[:, 0:w], func=AF.Exp, scale=-0.5),
                 [lnv_b[0]], [rstd_b[0]])
            db_ = dst_bufs_fn(c)
            if len(db_) == 1:
                db_ = db_ * 8
            for kc in range(8):
                S.op("dve", lambda e, kc=kc: e.scalar_tensor_tensor(
                    out=dst3(lo, hi)[:, kc, :], in0=src3(lo, hi)[:, kc, :], scalar=gvec(goff, kc), in1=rstd[:, 0:w],
                    op0=ALU.mult, op1=ALU.mult), [rstd_b[0]] + CB + sb_, [db_[kc]])

        return p_sq, p_ones, p_rest

    deferred = []

    def run_deferred():
        while deferred:
            lab = S.label
            S.label = lab.split("/")[0] + "/dnorm"
            deferred.pop(0)()
            S.label = lab

    def rmsnorm_T(src3, src_bufs_fn, ntok, goff, dst3, dst_bufs_fn, chunks=None, defer_last=False):
        nch = (ntok + SUB - 1) // SUB
        cl = list(range(nch) if chunks is None else chunks)
        for c in cl:
            pcs = norm_pieces(src3, src_bufs_fn, ntok, goff, dst3, dst_bufs_fn, c)
            if defer_last and c == cl[-1]:
                pcs[0]()
                deferred.append(pcs[1])
                deferred.append(pcs[2])
            else:
                for p in pcs:
                    p()

    def xsrc(lo, hi):
        return xT[:, :, lo:hi]

    def xbufs(c):
        return [xT_b[m][c] for m in range(8)]

    hdst = lambda lo, hi: hT[:, :, lo:hi]
    hdb = lambda c: hT_b[c]

    def norm_x_to_h(goff, chunks=None):
        S.label = S.label.split("/")[0] + "/norm"
        rmsnorm_T(xsrc, xbufs, T, goff, hdst, hdb, chunks=chunks, defer_last=True)

    def tail_hook(goff, ngroups):
        pcs = norm_pieces(xsrc, xbufs, T, goff, hdst, hdb, 0)

        def hook(s, gi):
            lab = S.label
            S.label = lab.split("/")[0] + "/tailnorm"
            if s == 0 and gi == ngroups:
                pcs[0]()
            if s == 1 and gi == 3:
                pcs[1]()
                pcs[2]()
            S.label = lab
        return hook

    def proj_multi(specs, nchunk, kch, rhs_fn, rhs_bufs_fn, evac, hook=None):
        loaded = []
        for (wv_cols, key) in specs:
            st, sbuf = wload(proj_parts(wv_cols, nchunk, kch), key=key)
            loaded.append((v3(st, kch, nchunk * 128), sbuf))
        for s in range(NSUB):
            gi = 0
            for si, (sv, sbuf) in enumerate(loaded):
                for m in range(nchunk):
                    bk, bb = psum()
                    S.mm([(bk[:], sv[:, kc, m * 128:(m + 1) * 128], rhs_fn(kc, s), kc == 0, kc == kch - 1)
                          for kc in range(kch)], [sbuf] + rhs_bufs_fn(s), [bb])
                    evac(si, m, s, bk, bb)
                    gi += 1
                    if s == 0 and gi == 3:
                        run_deferred()
                    if hook is not None:
                        hook(s, gi)


    def ffn_parts(l, jp):
        w1v, w3v = kview(w1_d[l]), kview(w3_d[l])
        j0 = jp * 2
        return [
            (lambda st: v3(st, 8, 512)[:, :, 0:256], w1v[:, :, j0 * 128:(j0 + 2) * 128]),
            (lambda st: v3(st, 8, 512)[:, :, 256:512], w3v[:, :, j0 * 128:(j0 + 2) * 128]),
        ]

    def ffn_pre(l):
        for jp in range(2):
            wpre(("ffn", l, jp), ffn_parts(l, jp))

    def ffn(l, next_pcs=None):
        w1v, w3v, w2v = kview(w1_d[l]), kview(w3_d[l]), kview(w2_d[l])
        S.label = f"ffn{l}/p1"
        pairs = [(0, 1), (2, 3), (4, 5), (6, 7), (8, 9), (10,)]
        for pair in pairs:
            loaded = [wload(ffn_parts(l, jp), key=("ffn", l, jp)) for jp in pair]
            for s in range(NSUB):
                for jp, (st, sbuf) in zip(pair, loaded):
                    sv = v3(st, 8, 512)
                    for jj in range(2):
                        j = jp * 2 + jj
                        rhs = lambda kc, s=s: hT[:, kc, s * SUB:(s + 1) * SUB]
                        b1, bb1 = psum()
                        S.mm([(b1[:], sv[:, kc, jj * 128:(jj + 1) * 128], rhs(kc), kc == 0, kc == 7) for kc in range(8)],
                             [sbuf] + hT_b[s], [bb1])
                        b3, bb3 = psum()
                        S.mm([(b3[:], sv[:, kc, 256 + jj * 128:256 + (jj + 1) * 128], rhs(kc), kc == 0, kc == 7)
                              for kc in range(8)], [sbuf] + hT_b[s], [bb3])
                        slt, slb = sl_t[(j * NSUB + s) % 2]
                        S.op("act", lambda e, b1=b1, slt=slt: e.activation(out=slt[:], in_=b1[:], func=AF.Silu),
                             [bb1], [slb[0]])
                        gb = gated_b[j * NSUB + s]
                        S.op("dve", lambda e, b3=b3, slt=slt, j=j, s=s: e.tensor_tensor(
                            out=gated[:, j, s * SUB:(s + 1) * SUB], in0=b3[:], in1=slt[:], op=ALU.mult),
                            [bb3, slb[0]], [gb])
                        if s == 0 and j == 1:
                            run_deferred()
        S.label = f"ffn{l}/p2"

        def w2parts(m):
            return [(lambda st: v3(st, NJ, 128), w2v[:, :, m * 128:(m + 1) * 128])]

        def p2group(m, s, st, sbuf):
            sv = v3(st, NJ, 128)
            bk, bb = psum()
            S.mm([(bk[:], sv[:, j, :], gated[:, j, s * SUB:(s + 1) * SUB], j == 0, j == NJ - 1)
                  for j in range(NJ)], [sbuf] + [gated_b[j * NSUB + s] for j in range(NJ)], [bb])
            S.op("dve", lambda e, bk=bk, m=m, s=s: e.scalar_tensor_tensor(
                out=xT[:, m, s * SUB:(s + 1) * SUB], in0=bk[:], scalar=0.5, in1=xT[:, m, s * SUB:(s + 1) * SUB],
                op0=ALU.mult, op1=ALU.add), [bb], [xT_b[m][s]])

        held = {}
        for m in range(8):
            st, sbuf = wload(w2parts(m))
            held[m] = (st, sbuf, last_slot[0])
            p2group(m, 0, st, sbuf)
        if next_pcs is not None:
            lab = S.label
            S.label = lab.split("/")[0] + "/tailnorm"
            next_pcs[0]()
            S.label = lab
        resident = set(range(8 - NSLOT, 8))
        reload_q = [m for m in range(8 - NSLOT - 1, -1, -1)]
        for idx, m in enumerate(range(7, -1, -1)):
            st, sbuf, si = held[m]
            p2group(m, 1, st, sbuf)
            if m in resident and reload_q:
                m2 = reload_q.pop(0)
                st2, sbuf2 = wload(w2parts(m2), slot=si)
                held[m2] = (st2, sbuf2, si)
            if idx == 1 and next_pcs is not None:
                lab = S.label
                S.label = lab.split("/")[0] + "/tailnorm"
                next_pcs[1]()
                next_pcs[2]()
                S.label = lab

    def proj_parts(wv_cols, nchunk, kch):
        return [(lambda st: v3(st, kch, nchunk * 128), wv_cols)]

    def proj_T(wv_cols, nchunk, kch, rhs_fn, rhs_bufs_fn, evac, key=None):
        st, sbuf = wload(proj_parts(wv_cols, nchunk, kch), key=key)
        sv = v3(st, kch, nchunk * 128)
        for m in range(nchunk):
            for s in range(NSUB):
                bk, bb = psum()
                S.mm([(bk[:], sv[:, kc, m * 128:(m + 1) * 128], rhs_fn(kc, s), kc == 0, kc == kch - 1)
                      for kc in range(kch)], [sbuf] + rhs_bufs_fn(s), [bb])
                evac(m, s, bk, bb)

    hrhs = lambda kc, s: hT[:, kc, s * SUB:(s + 1) * SUB]
    hrb = lambda s: hT_b[s]

    def sl(s):
        return slice(s * SUB, (s + 1) * SUB)

    def mem_prologue():
        S.label = "mem"
        for blk in range(2):
            S.dma("sp", [(stg[blk][:], mem_d[blk * 128:(blk + 1) * 128, :])], [], [stg_b[blk]], f"dma:stg{blk}")
            for half in range(2):
                bk, bb = psum()
                S.tr([(bk[:, i * 128:(i + 1) * 128], stg[blk][:, (half * 4 + i) * 128:(half * 4 + i + 1) * 128], identF)
                      for i in range(4)], [stg_b[blk]] + CB, [bb])
                S.op("act", lambda e, bk=bk, half=half, blk=blk: e.activation(
                    out=memT[:, half * 4:half * 4 + 4, blk * 128:(blk + 1) * 128],
                    in_=bk[:].rearrange("p (a b) -> p a b", a=4), func=AF.Copy), [bb], [memT_b[0]])
        rmsnorm_T(lambda lo, hi: memT[:, :, lo:hi], lambda c: [memT_b[0]], MEM, O_G_MEM,
                  lambda lo, hi: mhT[:, :, lo:hi], lambda c: [mhT_b[0]])
        wkv = kview(wk_d)
        for half in range(2):
            st, sbuf = wload([(lambda st: v3(st, 8, 512), wkv[:, :, half * 512:(half + 1) * 512])])
            sv = v3(st, 8, 512)
            for c in range(4):
                bk, bb = psum()
                S.mm([(bk[:, 0:MEM], sv[:, kc, c * 128:(c + 1) * 128], mhT[:, kc, :], kc == 0, kc == 7)
                      for kc in range(8)], [sbuf, mhT_b[0]], [bb])
                S.op("act", lambda e, bk=bk, half=half, c=c: e.activation(
                    out=KT[:, half * 4 + c, :], in_=bk[:, 0:MEM], func=AF.Copy), [bb], [KT_b])
        wvv = kview(wv_d)
        for half in range(2):
            st, sbuf = wload([(lambda st: v3(st, 8, 512), wvv[:, :, half * 512:(half + 1) * 512])])
            sv = v3(st, 8, 512)
            for mb in range(2):
                bk, bb = psum()
                S.mm([(bk[:], mhT[:, kc, mb * 128:(mb + 1) * 128], sv[:, kc, :], kc == 0, kc == 7)
                      for kc in range(8)], [sbuf, mhT_b[0]], [bb])
                S.op("act", lambda e, bk=bk, half=half, mb=mb: e.activation(
                    out=Vm[:, mb, half * 512:(half + 1) * 512], in_=bk[:], func=AF.Copy), [bb], [Vm_b])

    def prefetch_tile(t):
        for blk in range(T // 128):
            r0 = t * T + blk * 128
            S.dma("sp", [(xpf[blk][0][:], x_d[r0:r0 + 128, :])], [], [xpf[blk][1][0]], f"dma:xpf{blk}")

    def load_block(blk):
        src_t, src_b = xpf[blk]
        s = (blk * 128) // SUB
        for half in range(2):
            bk, bb = psum()
            S.tr([(bk[:, i * 128:(i + 1) * 128], src_t[:, (half * 4 + i) * 128:(half * 4 + i + 1) * 128], identF)
                  for i in range(4)], [src_b[0]] + CB, [bb])
            fn = lambda e, bk=bk, half=half, blk=blk: e.activation(
                out=xT[:, half * 4:half * 4 + 4, blk * 128:(blk + 1) * 128],
                in_=bk[:].rearrange("p (a b) -> p a b", a=4), func=AF.Copy)
            fn2 = lambda e, bk=bk, half=half, blk=blk: e.tensor_copy(
                out=xT[:, half * 4:half * 4 + 4, blk * 128:(blk + 1) * 128],
                in_=bk[:].rearrange("p (a b) -> p a b", a=4))
            wb = [xT_b[half * 4 + i][s] for i in range(4)]
            if half == 0:
                S.op("act", fn, [bb], wb)
            else:
                S.op("dve", fn2, [bb], wb)

    ost_rr = [0]

    def store_block(t, blk):
        r0 = t * T + blk * 128
        s = (blk * 128) // SUB
        for half in range(2):
            oi = ost_rr[0] % NOST
            ost_rr[0] += 1
            ot, ob = ost_o[oi]
            bk, bb = psum()
            S.tr([(bk[:, i * 128:(i + 1) * 128], xT[:, half * 4 + i, blk * 128:(blk + 1) * 128], identF)
                  for i in range(4)], [xT_b[half * 4 + i][s] for i in range(4)] + CB, [bb])
            if half == 0:
                S.op("act", lambda e, bk=bk, ot=ot: e.activation(out=ot[:], in_=bk[:], func=AF.Copy), [bb], [ob[0]])
            else:
                S.op("dve", lambda e, bk=bk, ot=ot: e.tensor_copy(out=ot[:], in_=bk[:]), [bb], [ob[0]])
            S.dma("sp", [(out_d[r0:r0 + 128, half * 512:(half + 1) * 512], ot[:])], [ob[0]], [], f"dma:out{oi}")

    def load_tile(t):
        S.label = "load"
        pcs = norm_pieces(xsrc, xbufs, T, O_G_FFN1, hdst, hdb, 0) if stage >= 1 else None
        for blk in range(T // 128):
            if pcs is not None and blk == 4:
                pcs[0]()
            if pcs is not None and blk == 6:
                pcs[1]()
                pcs[2]()
            load_block(blk)

    def store_tile(t, final_norm=True):
        S.label = "store"
        if final_norm:
            rmsnorm_T(xsrc, xbufs, T, O_G_FIN, xsrc, xbufs, chunks=(1,), defer_last=True)
        for blk in range(T // 128):
            store_block(t, blk)
            if blk == 1:
                run_deferred()

    def boundary(t):
        S.label = "store"
        rmsnorm_T(xsrc, xbufs, T, O_G_FIN, xsrc, xbufs, chunks=(1,), defer_last=True)
        pcs = norm_pieces(xsrc, xbufs, T, O_G_FFN1, hdst, hdb, 0)
        for blk in range(4):
            store_block(t, blk)
            if blk == 1:
                run_deferred()
        for i in range(4):
            S.label = "load"
            load_block(i)
            S.label = "store"
            store_block(t, 4 + i)
        S.label = "load"
        pcs[0]()
        for blk in range(4, 8):
            if blk == 6:
                pcs[1]()
                pcs[2]()
            load_block(blk)

    def proj_groups(wv_cols, nchunk, kch, rhs_fn, rhs_bufs_fn, evac):
        state = {}

        def load():
            st, sbuf = wload([(lambda st: v3(st, kch, nchunk * 128), wv_cols)])
            state["sv"] = v3(st, kch, nchunk * 128)
            state["sbuf"] = sbuf

        groups = []
        for m in range(nchunk):
            for s in range(NSUB):
                def g(m=m, s=s):
                    if "sv" not in state:
                        load()
                    sv, sbuf = state["sv"], state["sbuf"]
                    bk, bb = psum()
                    S.mm([(bk[:], sv[:, kc, m * 128:(m + 1) * 128], rhs_fn(kc, s), kc == 0, kc == kch - 1)
                          for kc in range(kch)], [sbuf] + rhs_bufs_fn(s), [bb])
                    evac(m, s, bk, bb)
                groups.append(g)
        return groups

    def mixer(t):
        S.label = "mix"
        winv = kview(win_d)
        wpre("mix_q", proj_parts(winv[:, :, 0:512], 4, 8))
        wpre("mix_k", proj_parts(winv[:, :, 512:1024], 4, 8))
        norm_x_to_h(O_G_MIX, chunks=(1,))
        S.label = "mix/proj1"
        def ev_qk(si, m, s, bk, bb):
            if si == 0:
                S.op("act", lambda e: e.activation(out=qT[:, m, sl(s)], in_=bk[:], func=AF.Copy, scale=128.0 ** -0.5),
                     [bb], [qT_b[s * 4 + i] for i in range(4)])
            else:
                S.op("dve", lambda e: e.tensor_copy(out=kTt[:, m, sl(s)], in_=bk[:]), [bb],
                     [kT_b[s * 4 + i] for i in range(4)])
        proj_multi([(winv[:, :, 0:512], "mix_q"), (winv[:, :, 512:1024], "mix_k")], 4, 8, hrhs, hrb, ev_qk)
        st, sbuf = wload([(lambda st: v3(st, 8, 16), winv[:, :, 3072:3088])])
        sv = v3(st, 8, 16)
        S.op("dve", lambda e: e.memset(aT[:], 1.0), [], [aT_b[0]])
        for s in range(NSUB):
            bk, bb = psum()
            S.mm([(bk[0:16, :], sv[:, kc, :], hT[:, kc, sl(s)], kc == 0, kc == 7) for kc in range(8)],
                 [sbuf] + hT_b[s], [bb])
            S.op("dve", lambda e, bk=bk, s=s: e.tensor_copy(out=aT[0:16, sl(s)], in_=bk[0:16, :]), [bb], [aT_b[0]])
        vslots = []
        for nh in range(2):
            st, sbuf = wload([(lambda st: v3(st, 8, 512), winv[:, :, 1024 + nh * 512:1024 + (nh + 1) * 512])])
            vslots.append((v3(st, 8, 512), sbuf))

        def v_group(blk, nh):
            def g():
                sv, sbuf = vslots[nh]
                bk, bb = psum()
                S.mm([(bk[:], hT[:, kc, blk * 128:(blk + 1) * 128], sv[:, kc, :], kc == 0, kc == 7) for kc in range(8)],
                     [sbuf] + hT_b[blk // 4], [bb])
                if (blk + nh) % 2 == 0:
                    S.op("act", lambda e: e.activation(out=vT[:, blk, nh * 512:(nh + 1) * 512], in_=bk[:],
                                                       func=AF.Copy), [bb], [v_b[blk]])
                else:
                    S.op("dve", lambda e: e.tensor_copy(out=vT[:, blk, nh * 512:(nh + 1) * 512], in_=bk[:]),
                         [bb], [v_b[blk]])
            return g

        r_lists = []
        for half in range(2):
            def ev_r(m, s, bk, bb, half=half):
                if (m + s) % 2 == 0:
                    S.op("dve", lambda e: e.tensor_copy(out=srT[:, half * 4 + m, sl(s)], in_=bk[:]),
                         [bb], [sr_b[s * 4 + i] for i in range(4)])
                else:
                    S.op("act", lambda e: e.activation(out=srT[:, half * 4 + m, sl(s)], in_=bk[:], func=AF.Copy),
                         [bb], [sr_b[s * 4 + i] for i in range(4)])
            r_lists.append(proj_groups(winv[:, :, 2048 + half * 512:2048 + (half + 1) * 512], 4, 8, hrhs, hrb, ev_r))

        def r_groups(s):
            return [r_lists[half][m * 2 + s] for half in range(2) for m in range(4)]

        fillA = [v_group(blk, nh) for blk in range(4) for nh in range(2)] + r_groups(0) + \
                [v_group(blk, nh) for blk in range(4, 8) for nh in range(2)]
        fillB = r_groups(1)

        def fill(lst, n=1):
            for _ in range(n):
                if lst:
                    lab = S.label
                    S.label = "mix/fill"
                    lst.pop(0)()
                    S.label = lab

        fillers = []
        for gi, (tt, tb) in enumerate([(tga, tga_b), (tgb, tgb_b)]):
            base = 3600 + gi * 1024
            for half in range(2):
                def ev_g(m, s, bk, bb, half=half, tt=tt, tb=tb):
                    S.op("act", lambda e: e.activation(out=tt[:, half * 4 + m, sl(s)], in_=bk[:], func=AF.Tanh, scale=0.5),
                         [bb], [tb[(half * 4 + m) * 2 + s]])
                fillers.extend(proj_groups(winv[:, :, base + half * 512:base + (half + 1) * 512], 4, 8, hrhs, hrb, ev_g))

        def filler(n=1):
            for _ in range(n):
                if fillers:
                    lab = S.label
                    S.label = "mix/gates"
                    fillers.pop(0)()
                    S.label = lab

        v4 = lambda ap: ap.rearrange("p (a b) -> p a b", a=4)

        def A1(c):
            p = c % 2
            zb, zb_b = zb2[p]
            lp, lp_b = lp2[p]
            tk = slice(c * 128, (c + 1) * 128)
            bz, bzb = psum()
            S.mm([(bz[:], aT[0:17, tk], walpha[:], True, True)], [aT_b[0]] + CB, [bzb])
            S.op("act", lambda e, bz=bz, zb=zb: e.activation(out=zb[:], in_=bz[:], func=AF.Exp, scale=-1.0),
                 [bzb], [zb_b[0]])
            S.op("act", lambda e, zb=zb, lp=lp: e.activation(out=lp[:], in_=zb[:], func=AF.Ln, bias=1.0),
                 [zb_b[0]], [lp_b[0]])

        def A2(c):
            p = c % 2
            lp, lp_b = lp2[p]
            eb, eb_b = eb2[p]
            enb, enb_b = enb2[p]
            tk = slice(c * 128, (c + 1) * 128)
            bc, bcb = psum()
            S.mm([(bc[:, h * 128:(h + 1) * 128], lp[:, h * 128:(h + 1) * 128], triB[:], True, True) for h in range(4)],
                 [lp_b[0]] + CB, [bcb])
            S.op("act", lambda e, bc=bc, eb=eb: e.activation(out=eb[:], in_=v4(bc[:]), func=AF.Exp, scale=-1.0 / 16.0),
                 [bcb], [eb_b[0]])
            S.op("act", lambda e, bc=bc, enb=enb: e.activation(out=enb[:], in_=v4(bc[:]), func=AF.Exp, scale=1.0 / 16.0),
                 [bcb], [enb_b[0]])
            S.op("dve", lambda e, tk=tk, eb=eb: e.tensor_tensor(out=qT[:, :, tk], in0=qT[:, :, tk], in1=eb[:], op=ALU.mult),
                 [eb_b[0]], [qT_b[c]])
            S.op("dve", lambda e, tk=tk, enb=enb: e.tensor_tensor(out=kTt[:, :, tk], in0=kTt[:, :, tk], in1=enb[:],
                                                                   op=ALU.mult), [enb_b[0]], [kT_b[c]])
            S.op("dve", lambda e, c=c, eb=eb: e.tensor_copy(out=ebl[:, c, :], in_=eb[:, :, 127]), [eb_b[0]], [ebl_b[c]])

        def A3(c):
            tk = slice(c * 128, (c + 1) * 128)
            bs, bsb = psum()
            S.mm([(bs[:, h * 128:(h + 1) * 128], kTt[:, h, tk], qT[:, h, tk], True, True) for h in range(4)],
                 [kT_b[c], qT_b[c]], [bsb])
            bt, btb = psum()
            btv = bt[:].bitcast(BF16)
            S.tr([(btv[:, h * 128:(h + 1) * 128], kTt[:, h, tk], identB[:]) for h in range(4)], [kT_b[c]] + CB, [btb])
            S.op("dve", lambda e, bs=bs, c=c: e.tensor_tensor(
                out=v4(smka[:, c, :]), in0=v4(bs[:]), in1=triB[:].unsqueeze(1).to_broadcast([128, 4, 128]),
                op=ALU.mult), [bsb] + CB, [smka_b[c]])
            S.op("act", lambda e, btv=btv, c=c: e.activation(out=kgTa[:, c, :], in_=btv[:, 0:512], func=AF.Copy),
                 [btb], [kgTa_b[c]])

        def B_kv(c):
            bbs = []
            for hh in range(2):
                bk, bb = psum_at(4 + hh)
                S.mm([(bk[:, h2 * 256:(h2 + 1) * 256], kgTa[:, c, (hh * 2 + h2) * 128:(hh * 2 + h2 + 1) * 128],
                       vT[:, c, (hh * 2 + h2) * 256:(hh * 2 + h2 + 1) * 256], True, True) for h2 in range(2)],
                     [kgTa_b[c], v_b[c]], [bb])
                bbs.append(bb)
            return bbs

        def B_o(c):
            tk = slice(c * 128, (c + 1) * 128)
            Sp = Sbf2[(c - 1) % 2]
            bbs = []
            for hh in range(2):
                bk, bb = psum_at((c % 2) * 2 + hh)
                items = []
                for h2 in range(2):
                    h = hh * 2 + h2
                    for vc in range(2):
                        col = (h2 * 2 + vc) * 128
                        items.append((bk[:, col:col + 128], vT[:, c, h * 256 + vc * 128:h * 256 + (vc + 1) * 128],
                                      smka[:, c, h * 128:(h + 1) * 128], True, False))
                        items.append((bk[:, col:col + 128], Sp[:, h, vc * 128:(vc + 1) * 128], qT[:, h, tk], False, True))
                S.mm(items, [v_b[c], smka_b[c]] + Sbf_b[(c - 1) % 2] + [qT_b[c]], [bb])
                bbs.append(bb)
            return bbs

        def o_view(c):
            b0 = (c % 2) * 2
            return psum_all[:, b0 * 512:(b0 + 2) * 512].rearrange("p (a b) -> p a b", a=8)

        def B_state(c, kvb):
            Sn = Sbf2[c % 2]
            kv = psum_all[:, 4 * 512:6 * 512].rearrange("p (a b) -> p a b", a=4)
            ebc = ebl[:, c, :].unsqueeze(2).to_broadcast([128, 4, 256])
            S.op("dve", lambda e: e.tensor_tensor(out=S32[:], in0=kv, in1=S32[:], op=ALU.add), kvb, S32_b)
            S.op("dve", lambda e: e.tensor_tensor(out=Sn[:], in0=S32[:], in1=ebc, op=ALU.mult),
                 [ebl_b[c]] + S32_b, Sbf_b[c % 2])
            S.op("dve", lambda e: e.tensor_tensor(out=S32[:], in0=S32[:], in1=ebc, op=ALU.mult), [ebl_b[c]], S32_b)

        def B_sq(c, ob):
            S.op("act", lambda e: e.activation(out=osq[:], in_=o_view(c), func=AF.Square), ob, [osq_b[0]])

        def B_ones(c):
            bn, bnb = psum_at(6)
            S.mm([(bn[:, h * 128:(h + 1) * 128], onesV[:], osq[:, h * 2 + vc, :], vc == 0, vc == 1)
                  for h in range(4) for vc in range(2)], [osq_b[0]] + CB, [bnb])
            return bn, bnb

        def B_lnexp(c, bnp):
            bn, bnb = bnp
            S.op("act", lambda e, bn=bn: e.activation(out=rsn[:], in_=v4(bn[:]), func=AF.Ln, bias=EPS), [bnb], [rsn_b[0]])
            S.op("act", lambda e: e.activation(out=rsn[:], in_=rsn[:], func=AF.Exp, scale=-0.5), [], [rsn_b[0]])

        def B_norm(c, ob):
            tk = slice(c * 128, (c + 1) * 128)
            ov_ = o_view(c).rearrange("p (h v) i -> p h v i", v=2)
            og_ = ogT[:, :, tk].rearrange("p (h v) i -> p h v i", v=2)
            for vc in range(2):
                S.op("dve", lambda e, vc=vc: e.scalar_tensor_tensor(
                    out=og_[:, :, vc, :], in0=ov_[:, :, vc, :], scalar=small[:, O_GHEAD + vc:O_GHEAD + vc + 1],
                    in1=rsn[:], op0=ALU.mult, op1=ALU.mult), ob + [rsn_b[0]] + CB, [og_b[c][vc]])

        S.label = "mix/glaA"
        for step in range(8 + 2):
            if step < 8:
                A1(step)
                fill(fillA)
            if 0 <= step - 1 < 8:
                A2(step - 1)
                fill(fillA)
            if 0 <= step - 2 < 8:
                A3(step - 2)
                fill(fillA)
        fill(fillA, len(fillA))
        S.label = "mix/glaB"
        psum_only7[0] = True
        prev = None
        for c in range(8):
            kvb = B_kv(c)
            B_state(c, kvb)
            ob = B_o(c)
            fill(fillB)
            if prev is not None:
                B_sq(*prev)
                bnp = B_ones(prev[0])
                B_lnexp(prev[0], bnp)
                B_norm(*prev)
            prev = (c, ob)
        fill(fillB, len(fillB))
        B_sq(*prev)
        bnp = B_ones(prev[0])
        B_lnexp(prev[0], bnp)
        B_norm(*prev)
        psum_only7[0] = False
        for s_ in range(NSUB):
            S.op("act", lambda e, s_=s_: e.activation(out=srT[:, :, sl(s_)], in_=srT[:, :, sl(s_)], func=AF.Silu),
                 [], [sr_b[s_ * 4 + i] for i in range(4)])
            S.op("dve", lambda e, s_=s_: e.tensor_tensor(out=ogT[:, :, sl(s_)], in0=ogT[:, :, sl(s_)],
                                                         in1=srT[:, :, sl(s_)], op=ALU.mult),
                 [sr_b[s_ * 4 + i] for i in range(4)], [b_ for i in range(4) for b_ in og_b[s_ * 4 + i]])

        S.label = "mix/pool"
        if t > 0:
            S.op("dve", lambda e: e.tensor_copy(out=uT[:, :, 0:16], in_=uhalo[:]), [uhalo_b], [uT_b[0]])
        else:
            S.op("dve", lambda e: e.memset(uT[:, :, 0:16], 0.0), [], [uT_b[0]])
        def ev_u(m, s, bk, bb):
            S.op("act", lambda e: e.activation(out=uT[:, m, 16 + s * SUB:16 + (s + 1) * SUB], in_=bk[:], func=AF.Copy),
                 [bb], [uT_b[0]])
        proj_T(winv[:, :, 3088:3600], 4, 8, hrhs, hrb, ev_u)
        S.op("dve", lambda e: e.tensor_copy(out=uhalo[:], in_=uT[:, :, T:T + 16]), [uT_b[0]], [uhalo_b])
        W = 16 + SUB
        for s in range(NSUB):
            for g in range(4):
                usrc = uT[:, g, s * SUB:s * SUB + W]
                cur_t, cur_b = None, None
                bufs = [(pwA, pwA_b[0]), (pwB, pwB_b[0])]
                for it in range(g + 1):
                    sh = 1 << it
                    v0 = 2 * sh - 1
                    dt_, db_ = bufs[it % 2]
                    if it == 0:
                        S.op("dve", lambda e, dt_=dt_, usrc=usrc, sh=sh, v0=v0: e.tensor_tensor(
                            out=dt_[:, v0:W], in0=usrc[:, v0:W], in1=usrc[:, v0 - sh:W - sh], op=ALU.add), [uT_b[0]], [db_])
                    else:
                        S.op("dve", lambda e, dt_=dt_, ct=cur_t, sh=sh, v0=v0: e.tensor_tensor(
                            out=dt_[:, v0:W], in0=ct[:, v0:W], in1=ct[:, v0 - sh:W - sh], op=ALU.add), [cur_b], [db_])
                    cur_t, cur_b = dt_, db_
                wdw = 2 << g
                S.op("dve", lambda e, ct=cur_t, usrc=usrc, g=g, s=s, wdw=wdw: e.scalar_tensor_tensor(
                    out=dfT[:, g, sl(s)], in0=ct[:, 16:W], scalar=1.0 / wdw, in1=usrc[:, 16:W],
                    op0=ALU.mult, op1=ALU.subtract), [cur_b, uT_b[0]], [df_b[g * 2 + s]])
                if t == 0 and s == 0:
                    S.op("dve", lambda e, ct=cur_t, g=g: e.tensor_tensor(
                        out=ct[:, 16:32], in0=ct[:, 16:32], in1=small[:, O_CINV + g * 16:O_CINV + (g + 1) * 16],
                        op=ALU.mult), [df_b[g * 2 + s]] + CB, [cur_b])
                    S.op("dve", lambda e, ct=cur_t, usrc=usrc, g=g: e.tensor_tensor(
                        out=dfT[:, g, 0:16], in0=ct[:, 16:32], in1=usrc[:, 16:32], op=ALU.subtract),
                        [cur_b, uT_b[0]], [df_b[g * 2 + s]])
        S.label = "mix/gates"
        filler(len(fillers))
        S.label = "mix/pool"
        st, sbuf = wload([(lambda st: v3(st, 4, 128), pmix_d.rearrange("g c d -> c g d"))])
        sv = v3(st, 4, 128)
        for g in range(4):
            for s in range(NSUB):
                bk, bb = psum()
                S.mm([(bk[:], sv[:, g, :], dfT[:, g, sl(s)], True, True)], [sbuf, df_b[g * 2 + s]], [bb])
                S.op("act", lambda e, bk=bk, g=g, s=s: e.activation(
                    out=zT[:, g, sl(s)], in_=bk[:], func=AF.Copy, scale=small[:, O_PSCALE + g:O_PSCALE + g + 1]),
                    [bb] + CB, [zT_b[g * 2 + s]])
        S.label = "mix/merge"
        wav = kview(wupa_d)
        wbv = kview(wupb_d)
        stb, sbufb = wload([(lambda st: v3(st, 4, 1024), wbv)])
        svb = v3(stb, 4, 1024)
        for half in range(2):
            sta, sbufa = wload([(lambda st: v3(st, 8, 512), wav[:, :, half * 512:(half + 1) * 512])])
            sva = v3(sta, 8, 512)
            for mm_ in range(4):
                m = half * 4 + mm_
                for s in range(NSUB):
                    ba, bab = psum()
                    S.mm([(ba[:], sva[:, kc, mm_ * 128:(mm_ + 1) * 128], ogT[:, kc, sl(s)], kc == 0, kc == 7)
                          for kc in range(8)], [sbufa] + [b_ for i in range(4) for b_ in og_b[s * 4 + i]], [bab])
                    bbk, bbb = psum()
                    S.mm([(bbk[:], svb[:, g, m * 128:(m + 1) * 128], zT[:, g, sl(s)], g == 0, g == 3)
                          for g in range(4)], [sbufb] + [zT_b[g * 2 + s] for g in range(4)], [bbb])
                    gbuf = tga_b[m * 2 + s]
                    S.op("dve", lambda e, ba=ba, m=m, s=s: e.scalar_tensor_tensor(
                        out=t1[:], in0=tga[:, m, sl(s)], scalar=1.0, in1=ba[:], op0=ALU.add, op1=ALU.mult),
                        [bab, gbuf], [t1_b[0]])
                    S.op("dve", lambda e, bbk=bbk, m=m, s=s: e.scalar_tensor_tensor(
                        out=tga[:, m, sl(s)], in0=tgb[:, m, sl(s)], scalar=1.0, in1=bbk[:], op0=ALU.add, op1=ALU.mult),
                        [bbb, tgb_b[m * 2 + s]], [gbuf])
                    S.op("dve", lambda e, m=m, s=s: e.tensor_tensor(
                        out=tga[:, m, sl(s)], in0=tga[:, m, sl(s)], in1=t1[:], op=ALU.add), [t1_b[0]], [gbuf])
        S.label = "mix/out"
        wmv = kview(wmix_d)

        def ev_o(si, m, s, bk, bb):
            mm_ = si * 4 + m
            S.op("dve", lambda e: e.scalar_tensor_tensor(
                out=xT[:, mm_, sl(s)], in0=bk[:], scalar=0.5, in1=xT[:, mm_, sl(s)], op0=ALU.mult, op1=ALU.add),
                [bb], [xT_b[mm_][s]])
        proj_multi([(wmv[:, :, half * 512:(half + 1) * 512], None) for half in range(2)], 4, 8,
                   lambda kc, s: tga[:, kc, sl(s)], lambda s: [tga_b[kc * 2 + s] for kc in range(8)], ev_o,
                   hook=(tail_hook(O_G_XA, 8) if stage >= 3 else None))

    def xattn():
        S.label = "xa"
        wqv = kview(wq_d)
        for half in range(2):
            wpre(("xa_q", half), proj_parts(wqv[:, :, half * 512:(half + 1) * 512], 4, 8))
        norm_x_to_h(O_G_XA, chunks=(1,))
        S.label = "xa/q"
        def ev_q(si, m, s, bk, bb):
            c = si * 4 + m
            if (m + s) % 2 == 0:
                S.op("act", lambda e: e.activation(out=xq[:, c, sl(s)], in_=bk[:], func=AF.Copy), [bb], [xq_b[c * 2 + s]])
            else:
                S.op("dve", lambda e: e.tensor_copy(out=xq[:, c, sl(s)], in_=bk[:]), [bb], [xq_b[c * 2 + s]])
        proj_multi([(wqv[:, :, half * 512:(half + 1) * 512], ("xa_q", half)) for half in range(2)], 4, 8, hrhs, hrb, ev_q)
        S.label = "xa/attn"
        its = [(h, s_) for h in range(4) for s_ in range(NSUB)]

        def scores(i):
            h, s_ = its[i]
            pms = []
            for mb in range(2):
                bk, bb = psum()
                S.mm([(bk[:], KT[:, h * 2 + dc, mb * 128:(mb + 1) * 128], xq[:, h * 2 + dc, sl(s_)], dc == 0, dc == 1)
                      for dc in range(2)], [KT_b, xq_b[(h * 2) * 2 + s_], xq_b[(h * 2 + 1) * 2 + s_]], [bb])
                pt, pb = Pm[(i % 2) * 2 + mb]
                S.op("act", lambda e, bk=bk, pt=pt: e.activation(out=pt[:], in_=bk[:], func=AF.Exp, scale=1.0 / 16.0),
                     [bb], [pb[0]])
                pms.append((pt, pb[0]))
            return pms

        nxt = scores(0)
        for i, (h, s_) in enumerate(its):
            pms = nxt
            if i + 1 < len(its):
                nxt = scores(i + 1)
            rz, rz_b = rz2[i % 2]
            bz, bzb = psum()
            S.mm([(bz[:], ones1[:], pms[mb][0][:], mb == 0, mb == 1) for mb in range(2)],
                 [pms[0][1], pms[1][1]] + CB, [bzb])
            S.op("dve", lambda e, bz=bz, rz=rz: e.reciprocal(out=rz[:], in_=bz[:]), [bzb], [rz_b[0]])
            for dvc in range(2):
                bk, bb = psum()
                S.mm([(bk[:], Vm[:, mb, h * 256 + dvc * 128:h * 256 + (dvc + 1) * 128], pms[mb][0][:], mb == 0, mb == 1)
                      for mb in range(2)], [Vm_b, pms[0][1], pms[1][1]], [bb])
                c = h * 2 + dvc
                S.op("dve", lambda e, bk=bk, c=c, s_=s_, rz=rz: e.tensor_tensor(
                    out=xo[:, c, sl(s_)], in0=bk[:], in1=rz[:], op=ALU.mult), [bb, rz_b[0]], [xo_b[c * 2 + s_]])
        S.label = "xa/o"
        wov = kview(wo_d)

        def ev_o(si, m, s, bk, bb):
            mm_ = si * 4 + m
            S.op("dve", lambda e: e.tensor_tensor(out=xT[:, mm_, sl(s)], in0=bk[:], in1=xT[:, mm_, sl(s)], op=ALU.add),
                 [bb], [xT_b[mm_][s]])
        proj_multi([(wov[:, :, half * 512:(half + 1) * 512], None) for half in range(2)], 4, 8,
                   lambda kc, s: xo[:, kc, sl(s)], lambda s: [xo_b[kc * 2 + s] for kc in range(8)], ev_o,
                   hook=(tail_hook(O_G_FFN2, 8) if stage >= 4 else None))

    mem_prologue()
    prefetch_tile(0)
    load_tile(0)
    for t in range(nt):
        if stage >= 1:
            S.label = "ffn0"
            ffn_pre(0)
            norm_x_to_h(O_G_FFN1, chunks=(1,))
            ffn(0, norm_pieces(xsrc, xbufs, T, O_G_MIX, hdst, hdb, 0) if stage >= 2 else None)
        if stage >= 2:
            mixer(t)
        if stage >= 3:
            xattn()
        if t + 1 < nt:
            prefetch_tile(t + 1)
        if stage >= 4:
            S.label = "ffn1"
            norm_x_to_h(O_G_FFN2, chunks=(1,))
            ffn(1, norm_pieces(xsrc, xbufs, T, O_G_FIN, xsrc, xbufs, 0) if stage >= 5 else None)
        if t + 1 < nt and stage >= 5:
            boundary(t)
        else:
            store_tile(t, final_norm=(stage >= 5))
            if t + 1 < nt:
                load_tile(t + 1)
    fin = []
    for key in [f"dma:out{i}" for i in range(8)]:
        if key in S.dmacnt:
            b = Buf(key)
            b.w = (key, S.dmacnt[key])
            fin.append(b)
    S.wait_all("sp", fin)
    S.emit()
    nc._pe_labels = S.pe_labels
    return nc


def make_small(inp):
    small = np.zeros((128, NS), np.float32)

    def gv(a):
        return np.ascontiguousarray(np.asarray(a, np.float32).reshape(-1, 128).T)

    small[:, O_G_FFN1:O_G_FFN1 + 8] = gv(inp["ffn1_norm"])
    small[:, O_G_MIX:O_G_MIX + 8] = gv(inp["mix_norm"])
    small[:, O_G_XA:O_G_XA + 8] = gv(inp["xa_norm"])
    small[:, O_G_MEM:O_G_MEM + 8] = gv(inp["mem_norm"])
    small[:, O_G_FFN2:O_G_FFN2 + 8] = gv(inp["ffn2_norm"])
    small[:, O_G_FIN:O_G_FIN + 8] = gv(inp["final_norm"])
    small[:, O_GHEAD:O_GHEAD + 2] = gv(inp["gla_head_norm"])
    small[:, O_PSCALE:O_PSCALE + 4] = gv(inp["pool_scale"])
    for g in range(4):
        w = 2 << g
        small[:, O_CINV + g * 16:O_CINV + (g + 1) * 16] = (1.0 / np.minimum(np.arange(1, 17), w))[None, :]
    small[:, O_BALPHA:O_BALPHA + 512] = np.asarray(inp["b_alpha"], np.float32).reshape(1, 512)
    small[:, O_IDENT:O_IDENT + 128] = np.eye(128, dtype=np.float32)
    small[:, O_TRI:O_TRI + 128] = np.triu(np.ones((128, 128), np.float32))
    return small


_NC_CACHE = {}


def kernel(**inputs):
    inp = {k: np.asarray(v) for k, v in inputs.items()}
    key = "full"
    if key not in _NC_CACHE:
        _NC_CACHE[key] = build_program()
    nc = _NC_CACHE[key]
    shared = {"small": make_small(inp), "b_alpha": np.ascontiguousarray(inp["b_alpha"], dtype=np.float32).reshape(1, 512)}
    for name in ("ffn1_w1", "ffn1_w3", "ffn1_w2", "ffn2_w1", "ffn2_w3", "ffn2_w2", "w_in", "w_alpha", "w_up_a",
                 "pool_mix", "w_up_b", "w_mix_out", "xa_wq", "xa_wk", "xa_wv", "xa_wo"):
        shared[name] = np.ascontiguousarray(inp[name][0], dtype=np.float32)
    in_maps = []
    for b in range(8):
        m = dict(shared)
        m["x"] = np.ascontiguousarray(inp["x"][b], dtype=np.float32)
        m["mem"] = np.ascontiguousarray(inp["mem"][b], dtype=np.float32)
        in_maps.append(m)
    res = run_bass_kernel_spmd(nc, in_maps, core_ids=list(range(8)))
    out = np.stack([np.asarray(r["out"], dtype=np.float32) for r in res.results], axis=0)
    return out
```

```python
import numpy as np
import concourse.bass as bass
import concourse.mybir as mybir
from concourse.bass_utils import run_bass_kernel_spmd

F32 = mybir.dt.float32
BF16 = mybir.dt.bfloat16
AF = mybir.ActivationFunctionType
ALU = mybir.AluOpType

D = 1024
SEQ = 4096
T = 1024
SUB = 512
NSUB = T // SUB
DFF = 2816
NJ = DFF // 128
MEM = 256
IN_TOTAL = 5648
EPS = 1e-6
NSLOT = 5
SLOT_ELEMS = 4096

O_G_FFN1, O_G_MIX, O_G_XA, O_G_MEM, O_G_FFN2, O_G_FIN = 0, 8, 16, 24, 32, 40
O_GHEAD, O_PSCALE, O_CINV, O_BALPHA, O_IDENT, O_TRI = 48, 50, 54, 118, 630, 758
NS = 886


class Buf:
    __slots__ = ("name", "w", "r", "alias")

    def __init__(self, name):
        self.name = name
        self.w = None
        self.r = {}
        self.alias = []


class Sched:
    def __init__(self, nc):
        self.nc = nc
        self.names = ["pe", "act", "dve", "pool", "sp"]
        self.semh = {k: nc.alloc_semaphore(name="sem_" + k) for k in self.names}
        self.cnt = {k: 0 for k in self.names}
        self.known = {k: {} for k in self.names}
        self.prog = {k: [] for k in self.names}
        self.dmacnt = {}
        self.label = ""
        self.pe_labels = []

    def dma_sem(self, key):
        if key not in self.semh:
            self.semh[key] = self.nc.alloc_semaphore(name="sem_" + key.replace(":", "_"))
            self.dmacnt[key] = 0
        return key

    def _deps(self, eng, reads, writes):
        deps = {}

        def add(kv):
            if kv is None:
                return
            k, v = kv
            if deps.get(k, 0) < v:
                deps[k] = v

        for b in reads:
            add(b.w)
        for b in writes:
            add(b.w)
            for k, v in b.r.items():
                add((k, v))
            for a in b.alias:
                add(a.w)
                for k, v in a.r.items():
                    add((k, v))
        for k, v in deps.items():
            if k == "pe" and eng == "pe":
                continue
            if self.known[eng].get(k, 0) < v:
                self.known[eng][k] = v
                self.prog[eng].append(("wait", k, v))

    def op(self, eng, fn, reads=(), writes=()):
        self._deps(eng, reads, writes)
        self.cnt[eng] += 1
        val = self.cnt[eng]
        self.prog[eng].append(("ins", fn, True))
        for b in reads:
            b.r[eng] = val
        for b in writes:
            b.w = (eng, val)
            b.r = {}
        return val

    def mm(self, items, reads, writes):
        eng = "pe"
        self._deps(eng, reads, writes)
        self.cnt[eng] += 1
        val = self.cnt[eng]
        n = len(items)
        self.pe_labels.extend([self.label] * n)
        for i, (o, l, r, st, sp) in enumerate(items):
            self.prog[eng].append(
                ("ins", (lambda e, o=o, l=l, r=r, st=st, sp=sp: e.matmul(o, l, r, start=st, stop=sp)), i == n - 1)
            )
        for b in reads:
            b.r[eng] = val
        for b in writes:
            b.w = (eng, val)
            b.r = {}

    def tr(self, items, reads, writes):
        eng = "pe"
        self._deps(eng, reads, writes)
        self.cnt[eng] += 1
        val = self.cnt[eng]
        n = len(items)
        self.pe_labels.extend([self.label] * n)
        for i, (o, a, idn) in enumerate(items):
            self.prog[eng].append(("ins", (lambda e, o=o, a=a, idn=idn: e.transpose(o, a, idn)), i == n - 1))
        for b in reads:
            b.r[eng] = val
        for b in writes:
            b.w = (eng, val)
            b.r = {}

    def dma(self, queue, pairs, reads, writes, key):
        self.dma_sem(key)
        self._deps(queue, reads, writes)
        for (o, i) in pairs:
            self.dmacnt[key] += 16
            self.prog[queue].append(("dma", (lambda e, o=o, i=i: e.dma_start(out=o, in_=i)), key))
        val = self.dmacnt[key]
        for b in reads:
            b.r[key] = val
        for b in writes:
            b.w = (key, val)
            b.r = {}

    def wait_all(self, eng, bufs):
        self._deps(eng, bufs, bufs)

    def emit(self):
        nc = self.nc
        engs = {"pe": "tensor", "act": "scalar", "dve": "vector", "pool": "gpsimd", "sp": "sync"}
        with nc.Block() as block:
            for name in self.names:
                prog = self.prog[name]
                own = self.semh[name]

                def body(e, prog=prog, own=own):
                    for item in prog:
                        if item[0] == "wait":
                            e.wait_ge(self.semh[item[1]], item[2])
                        elif item[0] == "ins":
                            ins = item[1](e)
                            if item[2]:
                                ins.then_inc(own, 1)
                        else:
                            ins = item[1](e)
                            ins.then_inc(self.semh[item[2]], 16)

                getattr(block, engs[name])(body)


def build_program(nt=SEQ // T, stage=99):
    nc = bass.Bass("TRN2", target_bir_lowering=False)
    S = Sched(nc)

    def din(name, shape):
        return nc.dram_tensor(name, list(shape), F32, kind="ExternalInput").ap()

    x_d = din("x", [SEQ, D])
    mem_d = din("mem", [MEM, D])
    small_d = din("small", [128, NS])
    w1_d = [din("ffn1_w1", [D, DFF]), din("ffn2_w1", [D, DFF])]
    w3_d = [din("ffn1_w3", [D, DFF]), din("ffn2_w3", [D, DFF])]
    w2_d = [din("ffn1_w2", [DFF, D]), din("ffn2_w2", [DFF, D])]
    win_d = din("w_in", [D, IN_TOTAL])
    walpha_d = din("w_alpha", [16, 512])
    balpha_d = din("b_alpha", [1, 512])
    wupa_d = din("w_up_a", [D, D])
    pmix_d = din("pool_mix", [4, 128, 128])
    wupb_d = din("w_up_b", [512, D])
    wmix_d = din("w_mix_out", [D, D])
    wq_d = din("xa_wq", [D, D])
    wk_d = din("xa_wk", [D, D])
    wv_d = din("xa_wv", [D, D])
    wo_d = din("xa_wo", [D, D])
    out_d = nc.dram_tensor("out", [SEQ, D], F32, kind="ExternalOutput").ap()

    def kview(w):
        return w.rearrange("(kc p) n -> p kc n", p=128)

    def sb(name, shape, dt):
        return nc.alloc_sbuf_tensor("sb_" + name, list(shape), dt)

    xT = sb("xT", [128, 8, T], F32)
    hT = sb("hT", [128, 8, T], BF16)
    slots = [sb(f"slot{i}", [128, SLOT_ELEMS], BF16) for i in range(NSLOT)]
    small = sb("small", [128, NS], F32)
    KT = sb("KT", [128, 8, MEM], BF16)
    Vm = sb("Vm", [128, 2, D], BF16)
    S32 = sb("S32", [128, 4, 256], F32)
    Sbf2 = [sb(f"Sbf{i}", [128, 4, 256], BF16) for i in range(2)]
    identB = sb("identB", [128, 128], BF16)
    triB = sb("triB", [128, 128], BF16)
    onesD = sb("onesD", [128, 128], BF16)
    onesV = sb("onesV", [128, 128], BF16)
    ones1 = sb("ones1", [128, 128], BF16)
    walpha = sb("walpha", [17, 512], BF16)
    uhalo = sb("uhalo", [128, 4, 16], F32)
    uhalo_b = Buf("uhalo")
    ARENA = 97 * 1024
    arena = sb("arena", [128, ARENA // 4], F32)
    abase = nc.lookup_mloc(arena).addr

    overlay = []

    def ov(name, off_k, shape, dt, nb=1):
        esz = 4 if dt == F32 else 2
        size = int(np.prod(shape[1:])) * esz
        lo = int(off_k * 1024)
        hi = lo + size
        assert hi <= ARENA, (name, hi)
        t = nc.alloc_sbuf_tensor_at("ov_" + name, list(shape), dt, offset=abase + lo)
        bufs = [Buf(f"{name}{i}") for i in range(nb)]
        for (l2, h2, b2) in overlay:
            if l2 < hi and lo < h2:
                for b in bufs:
                    for o in b2:
                        b.alias.append(o)
                        o.alias.append(b)
        overlay.append((lo, hi, bufs))
        return t, bufs

    stg_o = [ov(f"stg{i}", 16 + 4 * i, [128, D], F32) for i in range(2)]
    stg = [x[0] for x in stg_o]
    NOST = 8
    ost_o = [ov(f"ost{i}", 2 * i, [128, 512], F32) for i in range(NOST)]
    gated, gated_b = ov("gated", 0, [128, NJ, T], BF16, nb=NJ * NSUB)
    sl_t = [ov(f"sl{i}", 44 + i, [128, SUB], BF16) for i in range(2)]
    sq, sq_b = ov("sq", 46, [128, 8, SUB], BF16)
    lnv, lnv_b = ov("lnv", 54, [128, SUB], F32)
    rstd, rstd_b = ov("rstd", 56, [128, SUB], F32)
    ptmp, ptmp_b = ov("ptmp", 36, [128, 4, SUB], F32)
    xpf = [ov(f"xpf{i}", 58 + 4 * i, [128, D], F32) for i in range(8)]
    qT, qT_b = ov("qT", 0, [128, 4, T], BF16, nb=8)
    kTt, kT_b = ov("kT", 8, [128, 4, T], BF16, nb=8)
    vT, v_b = ov("vtok", 16, [128, 8, D], BF16, nb=8)
    srT, sr_b = ov("srT", 32, [128, 8, T], BF16, nb=8)
    aT, aT_b = ov("aT", 48, [32, T], BF16, nb=1)
    zb2 = [ov(f"zb{i}", 50 + 2 * i, [128, 512], F32) for i in range(2)]
    lp2 = [ov(f"lp{i}", 54 + i, [128, 512], BF16) for i in range(2)]
    enb2 = [ov(f"enb{i}", 56 + 2 * i, [128, 4, 128], F32) for i in range(2)]
    eb2 = [ov(f"eb{i}", 60 + 2 * i, [128, 4, 128], F32) for i in range(2)]
    kgTa, kgTa_b = ov("kgTa", 64, [128, 8, 512], BF16, nb=8)
    smka, smka_b = ov("smka", 72, [128, 8, 512], BF16, nb=8)
    ebl, ebl_b = ov("ebl", 80, [128, 8, 4], F32, nb=8)
    osq, osq_b = ov("osq", 50, [128, 8, 128], BF16)
    rsn, rsn_b = ov("rsn", 52, [128, 4, 128], F32)
    onT, onT_b = ov("onT", 54, [128, 8, 128], BF16)
    ogT, og_b2 = ov("ogT", 81, [128, 8, T], BF16, nb=16)
    og_b = [[og_b2[2 * c], og_b2[2 * c + 1]] for c in range(8)]
    tga, tga_b = ov("tga", 0, [128, 8, T], BF16, nb=16)
    tgb, tgb_b = ov("tgb", 16, [128, 8, T], BF16, nb=16)
    uT, uT_b = ov("uT", 32, [128, 4, 16 + T], F32, nb=1)
    pwA, pwA_b = ov("pwA", 48.25, [128, 16 + SUB], F32)
    pwB, pwB_b = ov("pwB", 50.5, [128, 16 + SUB], F32)
    dfT, df_b = ov("dfT", 52.75, [128, 4, T], BF16, nb=8)
    zT, zT_b = ov("zT", 60.75, [128, 4, T], BF16, nb=8)
    t1, t1_b = ov("t1", 69, [128, SUB], F32)
    xq, xq_b = ov("xq", 0, [128, 8, T], BF16, nb=16)
    xo, xo_b = ov("xo", 16, [128, 8, T], BF16, nb=16)
    Pm = [ov(f"Pm{i}", 32 + i, [128, SUB], BF16) for i in range(4)]
    rz2 = [ov(f"rz{i}", 36 + 2 * i, [128, SUB], F32) for i in range(2)]
    memT, memT_b = ov("memT", 0, [128, 8, MEM], F32)
    mhT, mhT_b = ov("mhT", 8, [128, 8, MEM], BF16)

    psum_all = nc.alloc_psum_tensor("psum_all", [128, 8 * 512], F32)
    banks = [psum_all[:, i * 512:(i + 1) * 512] for i in range(8)]
    bank_b = [Buf(f"bank{i}") for i in range(8)]
    bank_rr = [0]

    psum_only7 = [False]

    def psum():
        if psum_only7[0]:
            return banks[7], bank_b[7]
        i = bank_rr[0] % 8
        bank_rr[0] += 1
        return banks[i], bank_b[i]

    def psum_at(i):
        return banks[i], bank_b[i]

    xT_b = [[Buf(f"xT{m}_{s}") for s in range(NSUB)] for m in range(8)]
    hT_b = [[Buf(f"hT{s}_{kc}") for kc in range(8)] for s in range(NSUB)]
    slot_b = [Buf(f"slot{i}") for i in range(NSLOT)]
    stg_b = [x[1][0] for x in stg_o]
    const_b = Buf("const")
    KT_b, Vm_b = Buf("KT"), Buf("Vm")
    S32_b = [Buf("S32a"), Buf("S32b")]
    Sbf_b = [[Buf(f"Sbf{p}{hh}") for hh in range(2)] for p in range(2)]
    slot_rr = [0]

    pre_issued = {}

    def wload(parts, key=None, slot=None):
        if key is not None and key in pre_issued:
            return pre_issued.pop(key)
        if slot is None:
            i = slot_rr[0] % NSLOT
            slot_rr[0] += 1
        else:
            i = slot
        st = slots[i]
        S.dma("pool", [(vf(st), src) for (vf, src) in parts], [], [slot_b[i]], f"dma:slot{i}")
        last_slot[0] = i
        return st, slot_b[i]

    last_slot = [0]

    def wpre(key, parts):
        pre_issued[key] = wload(parts)

    def v3(st, a, b):
        return st[:, 0:a * b].rearrange("p (a b) -> p a b", a=a)

    S.dma("sp", [(small[:], small_d)], [], [const_b], "dma:const")
    wa_b = Buf("walpha")
    S.dma("pool", [(walpha[0:16, :], walpha_d), (walpha[16:17, :], balpha_d)], [], [wa_b], "dma:const2")
    cb2 = Buf("const2")
    S.op("dve", lambda e: e.tensor_copy(out=identB[:], in_=small[:, O_IDENT:O_IDENT + 128]), [const_b], [cb2])
    S.op("dve", lambda e: e.tensor_copy(out=triB[:], in_=small[:, O_TRI:O_TRI + 128]), [const_b], [cb2])
    S.op("dve", lambda e: e.memset(onesD[:], 1.0 / 1024.0), [], [cb2])
    S.op("dve", lambda e: e.memset(onesV[:], 1.0 / 256.0), [], [cb2])
    S.op("dve", lambda e: e.memset(ones1[:], 1.0), [], [cb2])
    S.op("dve", lambda e: e.memset(S32[:], 0.0), [], S32_b)
    for p_ in range(2):
        S.op("dve", lambda e, p_=p_: e.memset(Sbf2[p_][:], 0.0), [], Sbf_b[p_])
    CB = [const_b, cb2, wa_b]
    identF = small[:, O_IDENT:O_IDENT + 128]

    def gvec(off, kc):
        return small[:, off + kc:off + kc + 1]

    sq_b2 = [Buf("sqA"), Buf("sqB")]
    for b_ in sq_b2:
        b_.alias = sq_b[0].alias

    def norm_pieces(src3, src_bufs_fn, ntok, goff, dst3, dst_bufs_fn, c):
        lo, hi = c * SUB, min(ntok, (c + 1) * SUB)
        w = hi - lo
        sb_ = src_bufs_fn(c)
        st = {}

        def p_sq():
            for hf in range(2):
                S.op("act", lambda e, hf=hf: e.activation(
                    out=sq[:, hf * 4:hf * 4 + 4, 0:w], in_=src3(lo, hi)[:, hf * 4:hf * 4 + 4, :], func=AF.Square),
                    sb_, [sq_b2[hf]])

        def p_ones():
            bk, bb = psum()
            st["bk"], st["bb"] = bk, bb
            S.mm([(bk[:, 0:w], onesD[:], sq[:, kc, 0:w], kc == 0, kc == 7) for kc in range(8)],
                 sq_b2 + CB, [bb])

        def p_rest():
            bk, bb = st["bk"], st["bb"]
            S.op("act", lambda e: e.activation(out=lnv[:, 0:w], in_=bk[:, 0:w], func=AF.Ln, bias=EPS),
                 [bb], [lnv_b[0]])
            S.op("act", lambda e: e.activation(out=rstd[:, 0:w], in_=lnv[:, 0:w], func=AF.Exp, scale=-0.5),
                 [lnv_b[0]], [rstd_b[0]])
            db_ = dst_bufs_fn(c)
            if len(db_) == 1:
                db_ = db_ * 8
            for kc in range(8):
                S.op("dve", lambda e, kc=kc: e.scalar_tensor_tensor(
                    out=dst3(lo, hi)[:, kc, :], in0=src3(lo, hi)[:, kc, :], scalar=gvec(goff, kc), in1=rstd[:, 0:w],
                    op0=ALU.mult, op1=ALU.mult), [rstd_b[0]] + CB + sb_, [db_[kc]])

        return p_sq, p_ones, p_rest

    deferred = []

    def run_deferred():
        while deferred:
            lab = S.label
            S.label = lab.split("/")[0] + "/dnorm"
            deferred.pop(0)()
            S.label = lab

    def rmsnorm_T(src3, src_bufs_fn, ntok, goff, dst3, dst_bufs_fn, chunks=None, defer_last=False):
        nch = (ntok + SUB - 1) // SUB
        cl = list(range(nch) if chunks is None else chunks)
        for c in cl:
            pcs = norm_pieces(src3, src_bufs_fn, ntok, goff, dst3, dst_bufs_fn, c)
            if defer_last and c == cl[-1]:
                pcs[0]()
                deferred.append(pcs[1])
                deferred.append(pcs[2])
            else:
                for p in pcs:
                    p()

    def xsrc(lo, hi):
        return xT[:, :, lo:hi]

    def xbufs(c):
        return [xT_b[m][c] for m in range(8)]

    hdst = lambda lo, hi: hT[:, :, lo:hi]
    hdb = lambda c: hT_b[c]

    def norm_x_to_h(goff, chunks=None):
        S.label = S.label.split("/")[0] + "/norm"
        rmsnorm_T(xsrc, xbufs, T, goff, hdst, hdb, chunks=chunks, defer_last=True)

    def tail_hook(goff, ngroups):
        pcs = norm_pieces(xsrc, xbufs, T, goff, hdst, hdb, 0)

        def hook(s, gi):
            lab = S.label
            S.label = lab.split("/")[0] + "/tailnorm"
            if s == 0 and gi == ngroups:
                pcs[0]()
            if s == 1 and gi == 3:
                pcs[1]()
                pcs[2]()
            S.label = lab
        return hook

    def proj_multi(specs, nchunk, kch, rhs_fn, rhs_bufs_fn, evac, hook=None):
        loaded = []
        for (wv_cols, key) in specs:
            st, sbuf = wload(proj_parts(wv_cols, nchunk, kch), key=key)
            loaded.append((v3(st, kch, nchunk * 128), sbuf))
        for s in range(NSUB):
            gi = 0
            for si, (sv, sbuf) in enumerate(loaded):
                for m in range(nchunk):
                    bk, bb = psum()
                    S.mm([(bk[:], sv[:, kc, m * 128:(m + 1) * 128], rhs_fn(kc, s), kc == 0, kc == kch - 1)
                          for kc in range(kch)], [sbuf] + rhs_bufs_fn(s), [bb])
                    evac(si, m, s, bk, bb)
                    gi += 1
                    if s == 0 and gi == 3:
                        run_deferred()
                    if hook is not None:
                        hook(s, gi)


    def ffn_parts(l, jp):
        w1v, w3v = kview(w1_d[l]), kview(w3_d[l])
        j0 = jp * 2
        return [
            (lambda st: v3(st, 8, 512)[:, :, 0:256], w1v[:, :, j0 * 128:(j0 + 2) * 128]),
            (lambda st: v3(st, 8, 512)[:, :, 256:512], w3v[:, :, j0 * 128:(j0 + 2) * 128]),
        ]

    def ffn_pre(l):
        for jp in range(2):
            wpre(("ffn", l, jp), ffn_parts(l, jp))

    def ffn(l, next_pcs=None):
        w1v, w3v, w2v = kview(w1_d[l]), kview(w3_d[l]), kview(w2_d[l])
        S.label = f"ffn{l}/p1"
        pairs = [(0, 1), (2, 3), (4, 5), (6, 7), (8, 9), (10,)]
        for pair in pairs:
            loaded = [wload(ffn_parts(l, jp), key=("ffn", l, jp)) for jp in pair]
            for s in range(NSUB):
                for jp, (st, sbuf) in zip(pair, loaded):
                    sv = v3(st, 8, 512)
                    for jj in range(2):
                        j = jp * 2 + jj
                        rhs = lambda kc, s=s: hT[:, kc, s * SUB:(s + 1) * SUB]
                        b1, bb1 = psum()
                        S.mm([(b1[:], sv[:, kc, jj * 128:(jj + 1) * 128], rhs(kc), kc == 0, kc == 7) for kc in range(8)],
                             [sbuf] + hT_b[s], [bb1])
                        b3, bb3 = psum()
                        S.mm([(b3[:], sv[:, kc, 256 + jj * 128:256 + (jj + 1) * 128], rhs(kc), kc == 0, kc == 7)
                              for kc in range(8)], [sbuf] + hT_b[s], [bb3])
                        slt, slb = sl_t[(j * NSUB + s) % 2]
                        S.op("act", lambda e, b1=b1, slt=slt: e.activation(out=slt[:], in_=b1[:], func=AF.Silu),
                             [bb1], [slb[0]])
                        gb = gated_b[j * NSUB + s]
                        S.op("dve", lambda e, b3=b3, slt=slt, j=j, s=s: e.tensor_tensor(
                            out=gated[:, j, s * SUB:(s + 1) * SUB], in0=b3[:], in1=slt[:], op=ALU.mult),
                            [bb3, slb[0]], [gb])
                        if s == 0 and j == 1:
                            run_deferred()
        S.label = f"ffn{l}/p2"

        def w2parts(m):
            return [(lambda st: v3(st, NJ, 128), w2v[:, :, m * 128:(m + 1) * 128])]

        def p2group(m, s, st, sbuf):
            sv = v3(st, NJ, 128)
            bk, bb = psum()
            S.mm([(bk[:], sv[:, j, :], gated[:, j, s * SUB:(s + 1) * SUB], j == 0, j == NJ - 1)
                  for j in range(NJ)], [sbuf] + [gated_b[j * NSUB + s] for j in range(NJ)], [bb])
            S.op("dve", lambda e, bk=bk, m=m, s=s: e.scalar_tensor_tensor(
                out=xT[:, m, s * SUB:(s + 1) * SUB], in0=bk[:], scalar=0.5, in1=xT[:, m, s * SUB:(s + 1) * SUB],
                op0=ALU.mult, op1=ALU.add), [bb], [xT_b[m][s]])

        held = {}
        for m in range(8):
            st, sbuf = wload(w2parts(m))
            held[m] = (st, sbuf, last_slot[0])
            p2group(m, 0, st, sbuf)
        if next_pcs is not None:
            lab = S.label
            S.label = lab.split("/")[0] + "/tailnorm"
            next_pcs[0]()
            S.label = lab
        resident = set(range(8 - NSLOT, 8))
        reload_q = [m for m in range(8 - NSLOT - 1, -1, -1)]
        for idx, m in enumerate(range(7, -1, -1)):
            st, sbuf, si = held[m]
            p2group(m, 1, st, sbuf)
            if m in resident and reload_q:
                m2 = reload_q.pop(0)
                st2, sbuf2 = wload(w2parts(m2), slot=si)
                held[m2] = (st2, sbuf2, si)
            if idx == 1 and next_pcs is not None:
                lab = S.label
                S.label = lab.split("/")[0] + "/tailnorm"
                next_pcs[1]()
                next_pcs[2]()
                S.label = lab

    def proj_parts(wv_cols, nchunk, kch):
        return [(lambda st: v3(st, kch, nchunk * 128), wv_cols)]

    def proj_T(wv_cols, nchunk, kch, rhs_fn, rhs_bufs_fn, evac, key=None):
        st, sbuf = wload(proj_parts(wv_cols, nchunk, kch), key=key)
        sv = v3(st, kch, nchunk * 128)
        for m in range(nchunk):
            for s in range(NSUB):
                bk, bb = psum()
                S.mm([(bk[:], sv[:, kc, m * 128:(m + 1) * 128], rhs_fn(kc, s), kc == 0, kc == kch - 1)
                      for kc in range(kch)], [sbuf] + rhs_bufs_fn(s), [bb])
                evac(m, s, bk, bb)

    hrhs = lambda kc, s: hT[:, kc, s * SUB:(s + 1) * SUB]
    hrb = lambda s: hT_b[s]

    def sl(s):
        return slice(s * SUB, (s + 1) * SUB)

    def mem_prologue():
        S.label = "mem"
        wkv0, wvv0 = kview(wk_d), kview(wv_d)
        for half in range(2):
            wpre(("mem_k", half), [(lambda st: v3(st, 8, 512), wkv0[:, :, half * 512:(half + 1) * 512])])
        for half in range(2):
            wpre(("mem_v", half), [(lambda st: v3(st, 8, 512), wvv0[:, :, half * 512:(half + 1) * 512])])
        for blk in range(2):
            S.dma("sp", [(stg[blk][:], mem_d[blk * 128:(blk + 1) * 128, :])], [], [stg_b[blk]], f"dma:stg{blk}")
            for half in range(2):
                bk, bb = psum()
                S.tr([(bk[:, i * 128:(i + 1) * 128], stg[blk][:, (half * 4 + i) * 128:(half * 4 + i + 1) * 128], identF)
                      for i in range(4)], [stg_b[blk]] + CB, [bb])
                S.op("act", lambda e, bk=bk, half=half, blk=blk: e.activation(
                    out=memT[:, half * 4:half * 4 + 4, blk * 128:(blk + 1) * 128],
                    in_=bk[:].rearrange("p (a b) -> p a b", a=4), func=AF.Copy), [bb], [memT_b[0]])
        rmsnorm_T(lambda lo, hi: memT[:, :, lo:hi], lambda c: [memT_b[0]], MEM, O_G_MEM,
                  lambda lo, hi: mhT[:, :, lo:hi], lambda c: [mhT_b[0]])
        wkv = kview(wk_d)
        for half in range(2):
            st, sbuf = wload([(lambda st: v3(st, 8, 512), wkv[:, :, half * 512:(half + 1) * 512])], key=("mem_k", half))
            sv = v3(st, 8, 512)
            for c in range(4):
                bk, bb = psum()
                S.mm([(bk[:, 0:MEM], sv[:, kc, c * 128:(c + 1) * 128], mhT[:, kc, :], kc == 0, kc == 7)
                      for kc in range(8)], [sbuf, mhT_b[0]], [bb])
                S.op("act", lambda e, bk=bk, half=half, c=c: e.activation(
                    out=KT[:, half * 4 + c, :], in_=bk[:, 0:MEM], func=AF.Copy), [bb], [KT_b])
        wvv = kview(wv_d)
        for half in range(2):
            st, sbuf = wload([(lambda st: v3(st, 8, 512), wvv[:, :, half * 512:(half + 1) * 512])], key=("mem_v", half))
            sv = v3(st, 8, 512)
            for mb in range(2):
                bk, bb = psum()
                S.mm([(bk[:], mhT[:, kc, mb * 128:(mb + 1) * 128], sv[:, kc, :], kc == 0, kc == 7)
                      for kc in range(8)], [sbuf, mhT_b[0]], [bb])
                S.op("act", lambda e, bk=bk, half=half, mb=mb: e.activation(
                    out=Vm[:, mb, half * 512:(half + 1) * 512], in_=bk[:], func=AF.Copy), [bb], [Vm_b])

    def prefetch_tile(t):
        for blk in range(T // 128):
            r0 = t * T + blk * 128
            S.dma("sp", [(xpf[blk][0][:], x_d[r0:r0 + 128, :])], [], [xpf[blk][1][0]], f"dma:xpf{blk}")

    def load_block(blk):
        src_t, src_b = xpf[blk]
        s = (blk * 128) // SUB
        for half in range(2):
            bk, bb = psum()
            S.tr([(bk[:, i * 128:(i + 1) * 128], src_t[:, (half * 4 + i) * 128:(half * 4 + i + 1) * 128], identF)
                  for i in range(4)], [src_b[0]] + CB, [bb])
            fn = lambda e, bk=bk, half=half, blk=blk: e.activation(
                out=xT[:, half * 4:half * 4 + 4, blk * 128:(blk + 1) * 128],
                in_=bk[:].rearrange("p (a b) -> p a b", a=4), func=AF.Copy)
            fn2 = lambda e, bk=bk, half=half, blk=blk: e.tensor_copy(
                out=xT[:, half * 4:half * 4 + 4, blk * 128:(blk + 1) * 128],
                in_=bk[:].rearrange("p (a b) -> p a b", a=4))
            wb = [xT_b[half * 4 + i][s] for i in range(4)]
            if half == 0:
                S.op("act", fn, [bb], wb)
            else:
                S.op("dve", fn2, [bb], wb)

    ost_rr = [0]

    def store_block(t, blk):
        r0 = t * T + blk * 128
        s = (blk * 128) // SUB
        for half in range(2):
            oi = ost_rr[0] % NOST
            ost_rr[0] += 1
            ot, ob = ost_o[oi]
            bk, bb = psum()
            S.tr([(bk[:, i * 128:(i + 1) * 128], xT[:, half * 4 + i, blk * 128:(blk + 1) * 128], identF)
                  for i in range(4)], [xT_b[half * 4 + i][s] for i in range(4)] + CB, [bb])
            if half == 0:
                S.op("act", lambda e, bk=bk, ot=ot: e.activation(out=ot[:], in_=bk[:], func=AF.Copy), [bb], [ob[0]])
            else:
                S.op("dve", lambda e, bk=bk, ot=ot: e.tensor_copy(out=ot[:], in_=bk[:]), [bb], [ob[0]])
            S.dma("sp", [(out_d[r0:r0 + 128, half * 512:(half + 1) * 512], ot[:])], [ob[0]], [], f"dma:out{oi}")

    def load_tile(t):
        S.label = "load"
        pcs = norm_pieces(xsrc, xbufs, T, O_G_FFN1, hdst, hdb, 0) if stage >= 1 else None
        for blk in range(T // 128):
            if pcs is not None and blk == 4:
                pcs[0]()
            if pcs is not None and blk == 6:
                pcs[1]()
                pcs[2]()
            load_block(blk)

    def store_tile(t, final_norm=True):
        S.label = "store"
        if final_norm:
            rmsnorm_T(xsrc, xbufs, T, O_G_FIN, xsrc, xbufs, chunks=(1,), defer_last=True)
        for blk in range(T // 128):
            store_block(t, blk)
            if blk == 1:
                run_deferred()

    def boundary(t):
        S.label = "store"
        rmsnorm_T(xsrc, xbufs, T, O_G_FIN, xsrc, xbufs, chunks=(1,), defer_last=True)
        pcs = norm_pieces(xsrc, xbufs, T, O_G_FFN1, hdst, hdb, 0)
        for blk in range(4):
            store_block(t, blk)
            if blk == 1:
                run_deferred()
        for i in range(4):
            S.label = "load"
            load_block(i)
            S.label = "store"
            store_block(t, 4 + i)
        S.label = "load"
        pcs[0]()
        for blk in range(4, 8):
            if blk == 6:
                pcs[1]()
                pcs[2]()
            load_block(blk)

    def proj_groups(wv_cols, nchunk, kch, rhs_fn, rhs_bufs_fn, evac):
        state = {}

        def load():
            st, sbuf = wload([(lambda st: v3(st, kch, nchunk * 128), wv_cols)])
            state["sv"] = v3(st, kch, nchunk * 128)
            state["sbuf"] = sbuf

        groups = []
        for m in range(nchunk):
            for s in range(NSUB):
                def g(m=m, s=s):
                    if "sv" not in state:
                        load()
                    sv, sbuf = state["sv"], state["sbuf"]
                    bk, bb = psum()
                    S.mm([(bk[:], sv[:, kc, m * 128:(m + 1) * 128], rhs_fn(kc, s), kc == 0, kc == kch - 1)
                          for kc in range(kch)], [sbuf] + rhs_bufs_fn(s), [bb])
                    evac(m, s, bk, bb)
                groups.append(g)
        return groups

    def mixer(t):
        S.label = "mix"
        winv = kview(win_d)
        wpre("mix_q", proj_parts(winv[:, :, 0:512], 4, 8))
        wpre("mix_k", proj_parts(winv[:, :, 512:1024], 4, 8))
        norm_x_to_h(O_G_MIX, chunks=(1,))
        S.label = "mix/proj1"
        def ev_qk(si, m, s, bk, bb):
            if si == 0:
                S.op("act", lambda e: e.activation(out=qT[:, m, sl(s)], in_=bk[:], func=AF.Copy, scale=128.0 ** -0.5),
                     [bb], [qT_b[s * 4 + i] for i in range(4)])
            else:
                S.op("dve", lambda e: e.tensor_copy(out=kTt[:, m, sl(s)], in_=bk[:]), [bb],
                     [kT_b[s * 4 + i] for i in range(4)])
        proj_multi([(winv[:, :, 0:512], "mix_q"), (winv[:, :, 512:1024], "mix_k")], 4, 8, hrhs, hrb, ev_qk)
        st, sbuf = wload([(lambda st: v3(st, 8, 16), winv[:, :, 3072:3088])])
        sv = v3(st, 8, 16)
        S.op("dve", lambda e: e.memset(aT[:], 1.0), [], [aT_b[0]])
        for s in range(NSUB):
            bk, bb = psum()
            S.mm([(bk[0:16, :], sv[:, kc, :], hT[:, kc, sl(s)], kc == 0, kc == 7) for kc in range(8)],
                 [sbuf] + hT_b[s], [bb])
            S.op("dve", lambda e, bk=bk, s=s: e.tensor_copy(out=aT[0:16, sl(s)], in_=bk[0:16, :]), [bb], [aT_b[0]])
        vslots = []
        for nh in range(2):
            st, sbuf = wload([(lambda st: v3(st, 8, 512), winv[:, :, 1024 + nh * 512:1024 + (nh + 1) * 512])])
            vslots.append((v3(st, 8, 512), sbuf))

        def v_group(blk, nh):
            def g():
                sv, sbuf = vslots[nh]
                bk, bb = psum()
                S.mm([(bk[:], hT[:, kc, blk * 128:(blk + 1) * 128], sv[:, kc, :], kc == 0, kc == 7) for kc in range(8)],
                     [sbuf] + hT_b[blk // 4], [bb])
                if (blk + nh) % 2 == 0:
                    S.op("act", lambda e: e.activation(out=vT[:, blk, nh * 512:(nh + 1) * 512], in_=bk[:],
                                                       func=AF.Copy), [bb], [v_b[blk]])
                else:
                    S.op("dve", lambda e: e.tensor_copy(out=vT[:, blk, nh * 512:(nh + 1) * 512], in_=bk[:]),
                         [bb], [v_b[blk]])
            return g

        r_lists = []
        for half in range(2):
            def ev_r(m, s, bk, bb, half=half):
                if (m + s) % 2 == 0:
                    S.op("dve", lambda e: e.tensor_copy(out=srT[:, half * 4 + m, sl(s)], in_=bk[:]),
                         [bb], [sr_b[s * 4 + i] for i in range(4)])
                else:
                    S.op("act", lambda e: e.activation(out=srT[:, half * 4 + m, sl(s)], in_=bk[:], func=AF.Copy),
                         [bb], [sr_b[s * 4 + i] for i in range(4)])
            r_lists.append(proj_groups(winv[:, :, 2048 + half * 512:2048 + (half + 1) * 512], 4, 8, hrhs, hrb, ev_r))

        def r_groups(s):
            return [r_lists[half][m * 2 + s] for half in range(2) for m in range(4)]

        fillA = [v_group(blk, nh) for blk in range(4) for nh in range(2)] + r_groups(0) + \
                [v_group(blk, nh) for blk in range(4, 8) for nh in range(2)]
        fillB = r_groups(1)

        def fill(lst, n=1):
            for _ in range(n):
                if lst:
                    lab = S.label
                    S.label = "mix/fill"
                    lst.pop(0)()
                    S.label = lab

        fillers = []
        for gi, (tt, tb) in enumerate([(tga, tga_b), (tgb, tgb_b)]):
            base = 3600 + gi * 1024
            for half in range(2):
                def ev_g(m, s, bk, bb, half=half, tt=tt, tb=tb):
                    S.op("act", lambda e: e.activation(out=tt[:, half * 4 + m, sl(s)], in_=bk[:], func=AF.Tanh, scale=0.5),
                         [bb], [tb[(half * 4 + m) * 2 + s]])
                fillers.extend(proj_groups(winv[:, :, base + half * 512:base + (half + 1) * 512], 4, 8, hrhs, hrb, ev_g))

        def filler(n=1):
            for _ in range(n):
                if fillers:
                    lab = S.label
                    S.label = "mix/gates"
                    fillers.pop(0)()
                    S.label = lab

        v4 = lambda ap: ap.rearrange("p (a b) -> p a b", a=4)

        def A1(c):
            p = c % 2
            zb, zb_b = zb2[p]
            lp, lp_b = lp2[p]
            tk = slice(c * 128, (c + 1) * 128)
            bz, bzb = psum()
            S.mm([(bz[:], aT[0:17, tk], walpha[:], True, True)], [aT_b[0]] + CB, [bzb])
            S.op("act", lambda e, bz=bz, zb=zb: e.activation(out=zb[:], in_=bz[:], func=AF.Exp, scale=-1.0),
                 [bzb], [zb_b[0]])
            S.op("act", lambda e, zb=zb, lp=lp: e.activation(out=lp[:], in_=zb[:], func=AF.Ln, bias=1.0),
                 [zb_b[0]], [lp_b[0]])

        def A2(c):
            p = c % 2
            lp, lp_b = lp2[p]
            eb, eb_b = eb2[p]
            enb, enb_b = enb2[p]
            tk = slice(c * 128, (c + 1) * 128)
            bc, bcb = psum()
            S.mm([(bc[:, h * 128:(h + 1) * 128], lp[:, h * 128:(h + 1) * 128], triB[:], True, True) for h in range(4)],
                 [lp_b[0]] + CB, [bcb])
            S.op("act", lambda e, bc=bc, eb=eb: e.activation(out=eb[:], in_=v4(bc[:]), func=AF.Exp, scale=-1.0 / 16.0),
                 [bcb], [eb_b[0]])
            S.op("act", lambda e, bc=bc, enb=enb: e.activation(out=enb[:], in_=v4(bc[:]), func=AF.Exp, scale=1.0 / 16.0),
                 [bcb], [enb_b[0]])
            S.op("dve", lambda e, tk=tk, eb=eb: e.tensor_tensor(out=qT[:, :, tk], in0=qT[:, :, tk], in1=eb[:], op=ALU.mult),
                 [eb_b[0]], [qT_b[c]])
            S.op("dve", lambda e, tk=tk, enb=enb: e.tensor_tensor(out=kTt[:, :, tk], in0=kTt[:, :, tk], in1=enb[:],
                                                                   op=ALU.mult), [enb_b[0]], [kT_b[c]])
            S.op("dve", lambda e, c=c, eb=eb: e.tensor_copy(out=ebl[:, c, :], in_=eb[:, :, 127]), [eb_b[0]], [ebl_b[c]])

        def A3(c):
            tk = slice(c * 128, (c + 1) * 128)
            bs, bsb = psum()
            S.mm([(bs[:, h * 128:(h + 1) * 128], kTt[:, h, tk], qT[:, h, tk], True, True) for h in range(4)],
                 [kT_b[c], qT_b[c]], [bsb])
            bt, btb = psum()
            btv = bt[:].bitcast(BF16)
            S.tr([(btv[:, h * 128:(h + 1) * 128], kTt[:, h, tk], identB[:]) for h in range(4)], [kT_b[c]] + CB, [btb])
            S.op("dve", lambda e, bs=bs, c=c: e.tensor_tensor(
                out=v4(smka[:, c, :]), in0=v4(bs[:]), in1=triB[:].unsqueeze(1).to_broadcast([128, 4, 128]),
                op=ALU.mult), [bsb] + CB, [smka_b[c]])
            S.op("act", lambda e, btv=btv, c=c: e.activation(out=kgTa[:, c, :], in_=btv[:, 0:512], func=AF.Copy),
                 [btb], [kgTa_b[c]])

        def B_kv(c):
            bbs = []
            for hh in range(2):
                bk, bb = psum_at(4 + hh)
                S.mm([(bk[:, h2 * 256:(h2 + 1) * 256], kgTa[:, c, (hh * 2 + h2) * 128:(hh * 2 + h2 + 1) * 128],
                       vT[:, c, (hh * 2 + h2) * 256:(hh * 2 + h2 + 1) * 256], True, True) for h2 in range(2)],
                     [kgTa_b[c], v_b[c]], [bb])
                bbs.append(bb)
            return bbs

        def B_o(c):
            tk = slice(c * 128, (c + 1) * 128)
            Sp = Sbf2[(c - 1) % 2]
            bbs = []
            for hh in range(2):
                bk, bb = psum_at((c % 2) * 2 + hh)
                items = []
                for h2 in range(2):
                    h = hh * 2 + h2
                    for vc in range(2):
                        col = (h2 * 2 + vc) * 128
                        items.append((bk[:, col:col + 128], vT[:, c, h * 256 + vc * 128:h * 256 + (vc + 1) * 128],
                                      smka[:, c, h * 128:(h + 1) * 128], True, False))
                        items.append((bk[:, col:col + 128], Sp[:, h, vc * 128:(vc + 1) * 128], qT[:, h, tk], False, True))
                S.mm(items, [v_b[c], smka_b[c]] + Sbf_b[(c - 1) % 2] + [qT_b[c]], [bb])
                bbs.append(bb)
            return bbs

        def o_view(c):
            b0 = (c % 2) * 2
            return psum_all[:, b0 * 512:(b0 + 2) * 512].rearrange("p (a b) -> p a b", a=8)

        def B_state(c, kvb):
            Sn = Sbf2[c % 2]
            kv = psum_all[:, 4 * 512:6 * 512].rearrange("p (a b) -> p a b", a=4)
            ebc = ebl[:, c, :].unsqueeze(2).to_broadcast([128, 4, 256])
            S.op("dve", lambda e: e.tensor_tensor(out=S32[:], in0=kv, in1=S32[:], op=ALU.add), kvb, S32_b)
            S.op("dve", lambda e: e.tensor_tensor(out=Sn[:], in0=S32[:], in1=ebc, op=ALU.mult),
                 [ebl_b[c]] + S32_b, Sbf_b[c % 2])
            S.op("dve", lambda e: e.tensor_tensor(out=S32[:], in0=S32[:], in1=ebc, op=ALU.mult), [ebl_b[c]], S32_b)

        def B_sq(c, ob):
            S.op("act", lambda e: e.activation(out=osq[:], in_=o_view(c), func=AF.Square), ob, [osq_b[0]])

        def B_ones(c):
            bn, bnb = psum_at(6)
            S.mm([(bn[:, h * 128:(h + 1) * 128], onesV[:], osq[:, h * 2 + vc, :], vc == 0, vc == 1)
                  for h in range(4) for vc in range(2)], [osq_b[0]] + CB, [bnb])
            return bn, bnb

        def B_lnexp(c, bnp):
            bn, bnb = bnp
            S.op("act", lambda e, bn=bn: e.activation(out=rsn[:], in_=v4(bn[:]), func=AF.Ln, bias=EPS), [bnb], [rsn_b[0]])
            S.op("act", lambda e: e.activation(out=rsn[:], in_=rsn[:], func=AF.Exp, scale=-0.5), [], [rsn_b[0]])

        def B_norm(c, ob):
            tk = slice(c * 128, (c + 1) * 128)
            ov_ = o_view(c).rearrange("p (h v) i -> p h v i", v=2)
            og_ = ogT[:, :, tk].rearrange("p (h v) i -> p h v i", v=2)
            for vc in range(2):
                S.op("dve", lambda e, vc=vc: e.scalar_tensor_tensor(
                    out=og_[:, :, vc, :], in0=ov_[:, :, vc, :], scalar=small[:, O_GHEAD + vc:O_GHEAD + vc + 1],
                    in1=rsn[:], op0=ALU.mult, op1=ALU.mult), ob + [rsn_b[0]] + CB, [og_b[c][vc]])

        S.label = "mix/glaA"
        for step in range(8 + 2):
            if step < 8:
                A1(step)
                fill(fillA)
            if 0 <= step - 1 < 8:
                A2(step - 1)
                fill(fillA)
            if 0 <= step - 2 < 8:
                A3(step - 2)
                fill(fillA)
        fill(fillA, len(fillA))
        S.label = "mix/glaB"
        psum_only7[0] = True
        prev = None
        for c in range(8):
            kvb = B_kv(c)
            B_state(c, kvb)
            ob = B_o(c)
            fill(fillB)
            if prev is not None:
                B_sq(*prev)
                bnp = B_ones(prev[0])
                B_lnexp(prev[0], bnp)
                B_norm(*prev)
            prev = (c, ob)
        fill(fillB, len(fillB))
        B_sq(*prev)
        bnp = B_ones(prev[0])
        B_lnexp(prev[0], bnp)
        B_norm(*prev)
        psum_only7[0] = False
        for s_ in range(NSUB):
            S.op("act", lambda e, s_=s_: e.activation(out=srT[:, :, sl(s_)], in_=srT[:, :, sl(s_)], func=AF.Silu),
                 [], [sr_b[s_ * 4 + i] for i in range(4)])
            S.op("dve", lambda e, s_=s_: e.tensor_tensor(out=ogT[:, :, sl(s_)], in0=ogT[:, :, sl(s_)],
                                                         in1=srT[:, :, sl(s_)], op=ALU.mult),
                 [sr_b[s_ * 4 + i] for i in range(4)], [b_ for i in range(4) for b_ in og_b[s_ * 4 + i]])

        S.label = "mix/pool"
        if t > 0:
            S.op("dve", lambda e: e.tensor_copy(out=uT[:, :, 0:16], in_=uhalo[:]), [uhalo_b], [uT_b[0]])
        else:
            S.op("dve", lambda e: e.memset(uT[:, :, 0:16], 0.0), [], [uT_b[0]])
        def ev_u(m, s, bk, bb):
            S.op("act", lambda e: e.activation(out=uT[:, m, 16 + s * SUB:16 + (s + 1) * SUB], in_=bk[:], func=AF.Copy),
                 [bb], [uT_b[0]])
        proj_T(winv[:, :, 3088:3600], 4, 8, hrhs, hrb, ev_u)
        S.op("dve", lambda e: e.tensor_copy(out=uhalo[:], in_=uT[:, :, T:T + 16]), [uT_b[0]], [uhalo_b])
        W = 16 + SUB
        for s in range(NSUB):
            for g in range(4):
                usrc = uT[:, g, s * SUB:s * SUB + W]
                cur_t, cur_b = None, None
                bufs = [(pwA, pwA_b[0]), (pwB, pwB_b[0])]
                for it in range(g + 1):
                    sh = 1 << it
                    v0 = 2 * sh - 1
                    dt_, db_ = bufs[it % 2]
                    if it == 0:
                        S.op("dve", lambda e, dt_=dt_, usrc=usrc, sh=sh, v0=v0: e.tensor_tensor(
                            out=dt_[:, v0:W], in0=usrc[:, v0:W], in1=usrc[:, v0 - sh:W - sh], op=ALU.add), [uT_b[0]], [db_])
                    else:
                        S.op("dve", lambda e, dt_=dt_, ct=cur_t, sh=sh, v0=v0: e.tensor_tensor(
                            out=dt_[:, v0:W], in0=ct[:, v0:W], in1=ct[:, v0 - sh:W - sh], op=ALU.add), [cur_b], [db_])
                    cur_t, cur_b = dt_, db_
                wdw = 2 << g
                S.op("dve", lambda e, ct=cur_t, usrc=usrc, g=g, s=s, wdw=wdw: e.scalar_tensor_tensor(
                    out=dfT[:, g, sl(s)], in0=ct[:, 16:W], scalar=1.0 / wdw, in1=usrc[:, 16:W],
                    op0=ALU.mult, op1=ALU.subtract), [cur_b, uT_b[0]], [df_b[g * 2 + s]])
                if t == 0 and s == 0:
                    S.op("dve", lambda e, ct=cur_t, g=g: e.tensor_tensor(
                        out=ct[:, 16:32], in0=ct[:, 16:32], in1=small[:, O_CINV + g * 16:O_CINV + (g + 1) * 16],
                        op=ALU.mult), [df_b[g * 2 + s]] + CB, [cur_b])
                    S.op("dve", lambda e, ct=cur_t, usrc=usrc, g=g: e.tensor_tensor(
                        out=dfT[:, g, 0:16], in0=ct[:, 16:32], in1=usrc[:, 16:32], op=ALU.subtract),
                        [cur_b, uT_b[0]], [df_b[g * 2 + s]])
        S.label = "mix/gates"
        filler(len(fillers))
        S.label = "mix/pool"
        st, sbuf = wload([(lambda st: v3(st, 4, 128), pmix_d.rearrange("g c d -> c g d"))])
        sv = v3(st, 4, 128)
        for g in range(4):
            for s in range(NSUB):
                bk, bb = psum()
                S.mm([(bk[:], sv[:, g, :], dfT[:, g, sl(s)], True, True)], [sbuf, df_b[g * 2 + s]], [bb])
                S.op("act", lambda e, bk=bk, g=g, s=s: e.activation(
                    out=zT[:, g, sl(s)], in_=bk[:], func=AF.Copy, scale=small[:, O_PSCALE + g:O_PSCALE + g + 1]),
                    [bb] + CB, [zT_b[g * 2 + s]])
        S.label = "mix/merge"
        wav = kview(wupa_d)
        wbv = kview(wupb_d)
        stb, sbufb = wload([(lambda st: v3(st, 4, 1024), wbv)])
        svb = v3(stb, 4, 1024)
        for half in range(2):
            sta, sbufa = wload([(lambda st: v3(st, 8, 512), wav[:, :, half * 512:(half + 1) * 512])])
            sva = v3(sta, 8, 512)
            for mm_ in range(4):
                m = half * 4 + mm_
                for s in range(NSUB):
                    ba, bab = psum()
                    S.mm([(ba[:], sva[:, kc, mm_ * 128:(mm_ + 1) * 128], ogT[:, kc, sl(s)], kc == 0, kc == 7)
                          for kc in range(8)], [sbufa] + [b_ for i in range(4) for b_ in og_b[s * 4 + i]], [bab])
                    bbk, bbb = psum()
                    S.mm([(bbk[:], svb[:, g, m * 128:(m + 1) * 128], zT[:, g, sl(s)], g == 0, g == 3)
                          for g in range(4)], [sbufb] + [zT_b[g * 2 + s] for g in range(4)], [bbb])
                    gbuf = tga_b[m * 2 + s]
                    S.op("dve", lambda e, ba=ba, m=m, s=s: e.scalar_tensor_tensor(
                        out=t1[:], in0=tga[:, m, sl(s)], scalar=1.0, in1=ba[:], op0=ALU.add, op1=ALU.mult),
                        [bab, gbuf], [t1_b[0]])
                    S.op("dve", lambda e, bbk=bbk, m=m, s=s: e.scalar_tensor_tensor(
                        out=tga[:, m, sl(s)], in0=tgb[:, m, sl(s)], scalar=1.0, in1=bbk[:], op0=ALU.add, op1=ALU.mult),
                        [bbb, tgb_b[m * 2 + s]], [gbuf])
                    S.op("dve", lambda e, m=m, s=s: e.tensor_tensor(
                        out=tga[:, m, sl(s)], in0=tga[:, m, sl(s)], in1=t1[:], op=ALU.add), [t1_b[0]], [gbuf])
        S.label = "mix/out"
        wmv = kview(wmix_d)

        def ev_o(si, m, s, bk, bb):
            mm_ = si * 4 + m
            S.op("dve", lambda e: e.scalar_tensor_tensor(
                out=xT[:, mm_, sl(s)], in0=bk[:], scalar=0.5, in1=xT[:, mm_, sl(s)], op0=ALU.mult, op1=ALU.add),
                [bb], [xT_b[mm_][s]])
        proj_multi([(wmv[:, :, half * 512:(half + 1) * 512], None) for half in range(2)], 4, 8,
                   lambda kc, s: tga[:, kc, sl(s)], lambda s: [tga_b[kc * 2 + s] for kc in range(8)], ev_o,
                   hook=(tail_hook(O_G_XA, 8) if stage >= 3 else None))

    def xattn():
        S.label = "xa"
        wqv = kview(wq_d)
        for half in range(2):
            wpre(("xa_q", half), proj_parts(wqv[:, :, half * 512:(half + 1) * 512], 4, 8))
        norm_x_to_h(O_G_XA, chunks=(1,))
        S.label = "xa/q"
        def ev_q(si, m, s, bk, bb):
            c = si * 4 + m
            if (m + s) % 2 == 0:
                S.op("act", lambda e: e.activation(out=xq[:, c, sl(s)], in_=bk[:], func=AF.Copy), [bb], [xq_b[c * 2 + s]])
            else:
                S.op("dve", lambda e: e.tensor_copy(out=xq[:, c, sl(s)], in_=bk[:]), [bb], [xq_b[c * 2 + s]])
        proj_multi([(wqv[:, :, half * 512:(half + 1) * 512], ("xa_q", half)) for half in range(2)], 4, 8, hrhs, hrb, ev_q)
        S.label = "xa/attn"
        its = [(h, s_) for h in range(4) for s_ in range(NSUB)]

        def scores(i):
            h, s_ = its[i]
            pms = []
            for mb in range(2):
                bk, bb = psum()
                S.mm([(bk[:], KT[:, h * 2 + dc, mb * 128:(mb + 1) * 128], xq[:, h * 2 + dc, sl(s_)], dc == 0, dc == 1)
                      for dc in range(2)], [KT_b, xq_b[(h * 2) * 2 + s_], xq_b[(h * 2 + 1) * 2 + s_]], [bb])
                pt, pb = Pm[(i % 2) * 2 + mb]
                S.op("act", lambda e, bk=bk, pt=pt: e.activation(out=pt[:], in_=bk[:], func=AF.Exp, scale=1.0 / 16.0),
                     [bb], [pb[0]])
                pms.append((pt, pb[0]))
            return pms

        nxt = scores(0)
        for i, (h, s_) in enumerate(its):
            pms = nxt
            if i + 1 < len(its):
                nxt = scores(i + 1)
            rz, rz_b = rz2[i % 2]
            bz, bzb = psum()
            S.mm([(bz[:], ones1[:], pms[mb][0][:], mb == 0, mb == 1) for mb in range(2)],
                 [pms[0][1], pms[1][1]] + CB, [bzb])
            S.op("dve", lambda e, bz=bz, rz=rz: e.reciprocal(out=rz[:], in_=bz[:]), [bzb], [rz_b[0]])
            for dvc in range(2):
                bk, bb = psum()
                S.mm([(bk[:], Vm[:, mb, h * 256 + dvc * 128:h * 256 + (dvc + 1) * 128], pms[mb][0][:], mb == 0, mb == 1)
                      for mb in range(2)], [Vm_b, pms[0][1], pms[1][1]], [bb])
                c = h * 2 + dvc
                S.op("dve", lambda e, bk=bk, c=c, s_=s_, rz=rz: e.tensor_tensor(
                    out=xo[:, c, sl(s_)], in0=bk[:], in1=rz[:], op=ALU.mult), [bb, rz_b[0]], [xo_b[c * 2 + s_]])
        S.label = "xa/o"
        wov = kview(wo_d)

        def ev_o(si, m, s, bk, bb):
            mm_ = si * 4 + m
            S.op("dve", lambda e: e.tensor_tensor(out=xT[:, mm_, sl(s)], in0=bk[:], in1=xT[:, mm_, sl(s)], op=ALU.add),
                 [bb], [xT_b[mm_][s]])
        proj_multi([(wov[:, :, half * 512:(half + 1) * 512], None) for half in range(2)], 4, 8,
                   lambda kc, s: xo[:, kc, sl(s)], lambda s: [xo_b[kc * 2 + s] for kc in range(8)], ev_o,
                   hook=(tail_hook(O_G_FFN2, 8) if stage >= 4 else None))

    mem_prologue()
    prefetch_tile(0)
    load_tile(0)
    for t in range(nt):
        if stage >= 1:
            S.label = "ffn0"
            ffn_pre(0)
            norm_x_to_h(O_G_FFN1, chunks=(1,))
            ffn(0, norm_pieces(xsrc, xbufs, T, O_G_MIX, hdst, hdb, 0) if stage >= 2 else None)
        if stage >= 2:
            mixer(t)
        if stage >= 3:
            xattn()
        if t + 1 < nt:
            prefetch_tile(t + 1)
        if stage >= 4:
            S.label = "ffn1"
            norm_x_to_h(O_G_FFN2, chunks=(1,))
            ffn(1, norm_pieces(xsrc, xbufs, T, O_G_FIN, xsrc, xbufs, 0) if stage >= 5 else None)
        if t + 1 < nt and stage >= 5:
            boundary(t)
        else:
            store_tile(t, final_norm=(stage >= 5))
            if t + 1 < nt:
                load_tile(t + 1)
    fin = []
    for key in [f"dma:out{i}" for i in range(8)]:
        if key in S.dmacnt:
            b = Buf(key)
            b.w = (key, S.dmacnt[key])
            fin.append(b)
    S.wait_all("sp", fin)
    S.emit()
    nc._pe_labels = S.pe_labels
    return nc


def make_small(inp):
    small = np.zeros((128, NS), np.float32)

    def gv(a):
        return np.ascontiguousarray(np.asarray(a, np.float32).reshape(-1, 128).T)

    small[:, O_G_FFN1:O_G_FFN1 + 8] = gv(inp["ffn1_norm"])
    small[:, O_G_MIX:O_G_MIX + 8] = gv(inp["mix_norm"])
    small[:, O_G_XA:O_G_XA + 8] = gv(inp["xa_norm"])
    small[:, O_G_MEM:O_G_MEM + 8] = gv(inp["mem_norm"])
    small[:, O_G_FFN2:O_G_FFN2 + 8] = gv(inp["ffn2_norm"])
    small[:, O_G_FIN:O_G_FIN + 8] = gv(inp["final_norm"])
    small[:, O_GHEAD:O_GHEAD + 2] = gv(inp["gla_head_norm"])
    small[:, O_PSCALE:O_PSCALE + 4] = gv(inp["pool_scale"])
    for g in range(4):
        w = 2 << g
        small[:, O_CINV + g * 16:O_CINV + (g + 1) * 16] = (1.0 / np.minimum(np.arange(1, 17), w))[None, :]
    small[:, O_BALPHA:O_BALPHA + 512] = np.asarray(inp["b_alpha"], np.float32).reshape(1, 512)
    small[:, O_IDENT:O_IDENT + 128] = np.eye(128, dtype=np.float32)
    small[:, O_TRI:O_TRI + 128] = np.triu(np.ones((128, 128), np.float32))
    return small


_NC_CACHE = {}


def kernel(**inputs):
    inp = {k: np.asarray(v) for k, v in inputs.items()}
    key = "full"
    if key not in _NC_CACHE:
        _NC_CACHE[key] = build_program()
    nc = _NC_CACHE[key]
    shared = {"small": make_small(inp), "b_alpha": np.ascontiguousarray(inp["b_alpha"], dtype=np.float32).reshape(1, 512)}
    for name in ("ffn1_w1", "ffn1_w3", "ffn1_w2", "ffn2_w1", "ffn2_w3", "ffn2_w2", "w_in", "w_alpha", "w_up_a",
                 "pool_mix", "w_up_b", "w_mix_out", "xa_wq", "xa_wk", "xa_wv", "xa_wo"):
        shared[name] = np.ascontiguousarray(inp[name][0], dtype=np.float32)
    in_maps = []
    for b in range(8):
        m = dict(shared)
        m["x"] = np.ascontiguousarray(inp["x"][b], dtype=np.float32)
        m["mem"] = np.ascontiguousarray(inp["mem"][b], dtype=np.float32)
        in_maps.append(m)
    res = run_bass_kernel_spmd(nc, in_maps, core_ids=list(range(8)))
    out = np.stack([np.asarray(r["out"], dtype=np.float32) for r in res.results], axis=0)
    return out
```
